# Optimizing a Trainium2 kernel written in Bass

```python
import math
import jax, jax.numpy as jnp
from jax import lax
import numpy as np

D_MODEL = 1024
BATCH = 4
SEQ = 8192
DEPTH = 1

HEAD_DIM = 64
SWA_HEADS = 8
SWA_KV_HEADS = 2
SWA_WINDOW = 128
SB_HEADS = 8
BLOCK = 128
SWA_Q = SWA_HEADS * HEAD_DIM
SWA_KV = SWA_KV_HEADS * HEAD_DIM
SB_W = SB_HEADS * HEAD_DIM
MIX_WIDTH = SWA_Q + SB_W
IN_COLS = SWA_Q + 2 * SWA_KV + 3 * SB_W
PEER_HEADS = 8
PEER_KEY_DIM = 256
PEER_N_KEYS = 128
PEER_N_EXPERTS = PEER_N_KEYS * PEER_N_KEYS
PEER_TOPK = 16
PEER_BLOCK = 128
DEEPNORM_ALPHA = (2.0 * DEPTH) ** 0.25
DEEPNORM_BETA = (8.0 * DEPTH) ** -0.25
LN_EPS = 1e-5

kernel_name = "hybrid_swa_stickbreak_peer_deepnorm_adaln"


def alibi_slopes(n_heads):
    return jnp.asarray([2.0 ** (-8.0 * (i + 1) / n_heads) for i in range(n_heads)], dtype=jnp.float32)


def layer_norm(x, g, b):
    xf = x.astype(jnp.float32)
    mu = jnp.mean(xf, axis=-1, keepdims=True)
    var = jnp.mean(jnp.square(xf - mu), axis=-1, keepdims=True)
    y = (xf - mu) * lax.rsqrt(var + LN_EPS) * g.astype(jnp.float32) + b.astype(jnp.float32)
    return y.astype(x.dtype)


def rms_norm(x, g):
    xf = x.astype(jnp.float32)
    y = xf * lax.rsqrt(jnp.mean(jnp.square(xf), axis=-1, keepdims=True) + LN_EPS) * g.astype(jnp.float32)
    return y.astype(x.dtype)


def swa_sink_alibi(q, k, v, sinks):
    b, s, _, d = q.shape
    nb = s // BLOCK
    r = SWA_HEADS // SWA_KV_HEADS
    qb = q.reshape(b, nb, BLOCK, SWA_KV_HEADS, r, d)

    def band(t):
        tb = t.reshape(b, nb, BLOCK, SWA_KV_HEADS, d)
        prev = jnp.concatenate([jnp.zeros_like(tb[:, :1]), tb[:, :-1]], axis=1)
        return jnp.concatenate([prev, tb], axis=2)

    kb, vb = band(k), band(v)
    scores = jnp.einsum('bnqgrd,bnkgd->bngrqk', qb, kb,
                        preferred_element_type=jnp.float32) / math.sqrt(d)
    qi = jnp.arange(BLOCK)
    kj = jnp.arange(2 * BLOCK)
    dist = (BLOCK + qi[:, None] - kj[None, :]).astype(jnp.float32)
    blk = jnp.arange(nb)
    valid = ((dist >= 0) & (dist < SWA_WINDOW))[None] & \
            ((blk[:, None, None] > 0) | (kj >= BLOCK)[None, None, :])
    slopes = alibi_slopes(SWA_HEADS).reshape(SWA_KV_HEADS, r)
    bias = -slopes[:, :, None, None] * dist[None, None]
    scores = jnp.where(valid[None, :, None, None], scores + bias[None, None], -jnp.inf)
    sink = sinks.astype(jnp.float32).reshape(SWA_KV_HEADS, r)[None, None, :, :, None, None]
    m = jnp.maximum(jnp.max(scores, axis=-1, keepdims=True), sink)
    p = jnp.exp(scores - m)
    den = jnp.sum(p, axis=-1, keepdims=True) + jnp.exp(sink - m)
    out = jnp.einsum('bngrqk,bnkgd->bnqgrd', (p / den).astype(v.dtype), vb)
    return out.reshape(b, s, SWA_Q)


def stick_breaking(q, k, v):
    b, s, h, d = q.shape
    outs = []
    for i in range(s // BLOCK):
        end = (i + 1) * BLOCK
        qb = q[:, i * BLOCK:end]
        kb = k[:, :end]
        vb = v[:, :end]
        z = jnp.einsum('bqhd,bkhd->bhqk', qb, kb,
                       preferred_element_type=jnp.float32) / math.sqrt(d)
        qpos = i * BLOCK + jnp.arange(BLOCK)
        kpos = jnp.arange(end)
        causal = kpos[None, :] < qpos[:, None]
        log_1m = jnp.where(causal, jax.nn.log_sigmoid(-z), 0.0)
        between = lax.cumsum(log_1m, axis=3, reverse=True) - log_1m
        a = jnp.where(causal, jnp.exp(jax.nn.log_sigmoid(z) + between), 0.0)
        outs.append(jnp.einsum('bhqk,bkhd->bqhd', a.astype(v.dtype), vb))
    return jnp.concatenate(outs, axis=1).reshape(b, s, h * d)


def peer(xm, w_q, sub_keys, u, v):
    b, s, dm = xm.shape
    q = jnp.einsum('bsd,dk->bsk', xm, w_q).reshape(b, s, PEER_HEADS, 2, PEER_KEY_DIM // 2)
    sc = jnp.einsum('bshpc,hpnc->bshpn', q, sub_keys, preferred_element_type=jnp.float32)
    top_s, top_i = lax.top_k(sc, PEER_TOPK)
    cand = top_s[..., 0, :, None] + top_s[..., 1, None, :]
    cand = cand.reshape(b, s, PEER_HEADS, PEER_TOPK * PEER_TOPK)
    best_s, best_c = lax.top_k(cand, PEER_TOPK)
    i1 = jnp.take_along_axis(top_i[..., 0, :], best_c // PEER_TOPK, axis=-1)
    i2 = jnp.take_along_axis(top_i[..., 1, :], best_c % PEER_TOPK, axis=-1)
    expert = i1 * PEER_N_KEYS + i2
    g = jax.nn.softmax(best_s, axis=-1)
    n_blk = (b * s) // PEER_BLOCK
    e_per_tok = PEER_HEADS * PEER_TOPK
    xs = xm.reshape(n_blk, PEER_BLOCK, dm)
    es = expert.reshape(n_blk, PEER_BLOCK, e_per_tok)
    gs = g.reshape(n_blk, PEER_BLOCK, e_per_tok)

    def block(args):
        xb, eb, gb = args
        ub = u[eb]
        hb = jax.nn.gelu(jnp.einsum('pd,ped->pe', xb, ub, preferred_element_type=jnp.float32))
        return jnp.einsum('pe,ped->pd', (gb * hb).astype(v.dtype), v[eb])

    y = lax.map(block, (xs, es, gs))
    return y.reshape(b, s, dm)


def setup_inputs(seed: int = 0) -> dict:
    key = jax.random.key(seed)
    ks = jax.random.split(key, 20)
    L, D = DEPTH, D_MODEL
    f32 = jnp.float32
    x = jax.random.normal(ks[0], (BATCH, SEQ, D), f32)
    c = jax.random.normal(ks[1], (BATCH, D), f32)
    w_ada = jax.random.normal(ks[2], (L, D, 6 * D), f32) * (0.1 * D ** -0.5)
    gate_pattern = jnp.concatenate([jnp.zeros((2 * D,), f32), jnp.ones((D,), f32)])
    b_ada = jnp.tile(gate_pattern, 2)[None] + 0.02 * jax.random.normal(ks[3], (L, 6 * D), f32)
    w_in = jax.random.normal(ks[4], (L, D, IN_COLS), f32) * D ** -0.5
    swa_sinks = 0.5 * jax.random.normal(ks[5], (L, SWA_HEADS), f32)
    group_norm_a = 1.0 + 0.02 * jax.random.normal(ks[6], (L, SWA_Q), f32)
    group_norm_b = 1.0 + 0.02 * jax.random.normal(ks[7], (L, SB_W), f32)
    w_out = jax.random.normal(ks[8], (L, MIX_WIDTH, D), f32) * (MIX_WIDTH ** -0.5 * DEEPNORM_BETA)
    ln1_g = 1.0 + 0.02 * jax.random.normal(ks[9], (L, D), f32)
    ln1_b = 0.02 * jax.random.normal(ks[10], (L, D), f32)
    peer_w_q = jax.random.normal(ks[11], (L, D, PEER_HEADS * PEER_KEY_DIM), f32) * D ** -0.5
    peer_sub_keys = jax.random.normal(ks[12], (L, PEER_HEADS, 2, PEER_N_KEYS, PEER_KEY_DIM // 2), f32) \
        * (PEER_KEY_DIM // 2) ** -0.5
    peer_u = jax.random.normal(ks[13], (L, PEER_N_EXPERTS, D), f32) * D ** -0.5
    peer_v = jax.random.normal(ks[14], (L, PEER_N_EXPERTS, D), f32) \
        * ((PEER_HEADS * PEER_TOPK) ** -0.5 * DEEPNORM_BETA)
    ln2_g = 1.0 + 0.02 * jax.random.normal(ks[15], (L, D), f32)
    ln2_b = 0.02 * jax.random.normal(ks[16], (L, D), f32)
    return {"x": x, "c": c, "w_ada": w_ada, "b_ada": b_ada, "w_in": w_in,
            "swa_sinks": swa_sinks, "group_norm_a": group_norm_a, "group_norm_b": group_norm_b,
            "w_out": w_out, "ln1_g": ln1_g, "ln1_b": ln1_b, "peer_w_q": peer_w_q,
            "peer_sub_keys": peer_sub_keys, "peer_u": peer_u, "peer_v": peer_v,
            "ln2_g": ln2_g, "ln2_b": ln2_b}


def reference(x, c, w_ada, b_ada, w_in, swa_sinks, group_norm_a, group_norm_b, w_out,
              ln1_g, ln1_b, peer_w_q, peer_sub_keys, peer_u, peer_v, ln2_g, ln2_b):
    b, s, _ = x.shape
    split_pts = [SWA_Q, SWA_Q + SWA_KV, SWA_Q + 2 * SWA_KV,
                 SWA_Q + 2 * SWA_KV + SB_W, SWA_Q + 2 * SWA_KV + 2 * SB_W]
    for l in range(DEPTH):
        ada = jnp.einsum('bd,de->be', jax.nn.silu(c), w_ada[l]) + b_ada[l]
        shift1, scale1, gate1, shift2, scale2, gate2 = [t[:, None, :] for t in jnp.split(ada, 6, axis=-1)]

        h = x * (1.0 + scale1) + shift1
        proj = jnp.einsum('bsd,dc->bsc', h, w_in[l])
        qa, ka, va, qb, kb, vb = jnp.split(proj, split_pts, axis=-1)
        oa = swa_sink_alibi(qa.reshape(b, s, SWA_HEADS, HEAD_DIM),
                            ka.reshape(b, s, SWA_KV_HEADS, HEAD_DIM),
                            va.reshape(b, s, SWA_KV_HEADS, HEAD_DIM), swa_sinks[l])
        ob = stick_breaking(qb.reshape(b, s, SB_HEADS, HEAD_DIM),
                            kb.reshape(b, s, SB_HEADS, HEAD_DIM),
                            vb.reshape(b, s, SB_HEADS, HEAD_DIM))
        o = jnp.concatenate([rms_norm(oa, group_norm_a[l]), rms_norm(ob, group_norm_b[l])], axis=-1)
        mix = jnp.einsum('bsc,cd->bsd', o, w_out[l])
        x = layer_norm(DEEPNORM_ALPHA * x + gate1 * mix, ln1_g[l], ln1_b[l])

        h = x * (1.0 + scale2) + shift2
        y = peer(h, peer_w_q[l], peer_sub_keys[l], peer_u[l], peer_v[l])
        x = layer_norm(DEEPNORM_ALPHA * x + gate2 * y, ln2_g[l], ln2_b[l])
    return x
```

```python
import numpy as np
import ml_dtypes
import concourse.bass as bass
import concourse.mybir as mybir
from concourse.bass_utils import run_bass_kernel_spmd

F32 = mybir.dt.float32
BF16 = mybir.dt.bfloat16
U32 = mybir.dt.uint32
AF = mybir.ActivationFunctionType
ALU = mybir.AluOpType
AX = mybir.AxisListType
ENGS = ["pe", "act", "dve", "pool", "sp"]
D = 1024
ALPHA = 2.0 ** 0.25
EPS = 1e-5
NEG = -30000.0


class Buf:
    __slots__ = ("name", "lw", "rd", "dsem", "dcnt")

    def __init__(self, name):
        self.name = name
        self.lw = None
        self.rd = []
        self.dsem = {}
        self.dcnt = {}


class Op:
    __slots__ = ("eng", "fn", "idx", "inc", "is_dma", "dsem", "dval", "waits", "cnt")


class Prog:
    def __init__(self):
        self.ops = {e: [] for e in ENGS}
        self.ndsem = 0
        self.clock = {e: {} for e in ENGS}
        self.dmas = []

    def _mk(self, eng, fn, reads, writes, is_dma, extra=()):
        o = Op()
        o.eng = eng
        o.fn = fn
        o.idx = len(self.ops[eng])
        o.inc = False
        o.is_dma = is_dma
        o.dsem = None
        o.dval = 0
        deps = list(extra)
        for b in reads:
            if b.lw is not None:
                deps.append(b.lw)
        for b in writes:
            if b.lw is not None:
                deps.append(b.lw)
            deps.extend(b.rd)
        ck = self.clock[eng]
        waits = []
        for d in deps:
            if d.is_dma:
                key = ("d", d.dsem)
                if ck.get(key, 0) >= d.dval:
                    continue
                ck[key] = d.dval
                waits.append(("d", d.dsem, d.dval))
            else:
                if d.eng == eng and eng == "pe":
                    continue
                if ck.get(d.eng, -1) >= d.idx:
                    continue
                ck[d.eng] = d.idx
                d.inc = True
                waits.append(("e", d.eng, d))
        o.waits = waits
        for b in reads:
            b.rd.append(o)
        for b in writes:
            b.lw = o
            b.rd = []
        self.ops[eng].append(o)
        return o

    def op(self, eng, fn, reads=(), writes=()):
        return self._mk(eng, fn, reads, writes, False)

    def dma(self, eng, out_ap, in_ap, sb, reads=(), writes=()):
        def fn(e):
            return e.dma_start(out=out_ap, in_=in_ap)
        o = self._mk(eng, fn, reads, writes, True)
        kind = "sw" if eng == "pool" else "hw"
        if kind not in sb.dsem:
            sb.dsem[kind] = self.ndsem
            sb.dcnt[kind] = 0
            self.ndsem += 1
        sb.dcnt[kind] += 16
        o.dsem = sb.dsem[kind]
        o.dval = sb.dcnt[kind]
        self.dmas.append(o)
        return o

    def barrier(self):
        lasts = [self.ops[e][-1] for e in ENGS if self.ops[e] and not self.ops[e][-1].is_dma]
        lasts = []
        for e in ENGS:
            for o in reversed(self.ops[e]):
                if not o.is_dma and o.fn is not None:
                    lasts.append(o)
                    break
        best = {}
        for o in self.dmas:
            if o.dsem not in best or best[o.dsem].dval < o.dval:
                best[o.dsem] = o
        dm = list(best.values())
        self.dmas = []
        for e in ENGS:
            self._mk(e, None, (), (), False, extra=lasts + dm)

    def emit(self, nc):
        self.barrier()
        for e in ENGS:
            c = 0
            for o in self.ops[e]:
                if o.inc and not o.is_dma:
                    c += 1
                o.cnt = c
        import contextlib
        with contextlib.ExitStack() as st:
            esem = {e: st.enter_context(nc.semaphore("s_" + e)) for e in ENGS}
            dsem = [st.enter_context(nc.semaphore("d%d" % i)) for i in range(self.ndsem)]
            block = st.enter_context(nc.Block())
            ops = self.ops

            def run(eng_name, e):
                for o in ops[eng_name]:
                    for w in o.waits:
                        if w[0] == "d":
                            e.wait_ge(dsem[w[1]], w[2])
                        else:
                            e.wait_ge(esem[w[1]], w[2].cnt)
                    if o.fn is None:
                        continue
                    ins = o.fn(e)
                    if o.is_dma:
                        ins.then_inc(dsem[o.dsem], 16)
                    elif o.inc:
                        ins.then_inc(esem[eng_name], 1)

            @block.tensor
            def _(e):
                run("pe", e)

            @block.scalar
            def _(e):
                run("act", e)

            @block.vector
            def _(e):
                run("dve", e)

            @block.gpsimd
            def _(e):
                run("pool", e)

            @block.sync
            def _(e):
                run("sp", e)


def V(a, off, dims):
    return bass.AP(tensor=a.tensor, offset=a.offset + off, ap=[list(a.ap[0])] + [list(d) for d in dims])


def build(NSB, NE=16384, dbg=False):
    S = 512 * NSB
    NOWN = NSB // 2
    T = 512 * NOWN
    NKB = S // 128
    NCH = NE // 128
    nc = bass.Bass("TRN2", target_bir_lowering=False)
    P = Prog()

    def din(name, shape, dt=F32):
        return nc.dram_tensor(name, shape, dt, kind="ExternalInput").ap()

    xall = din("xall", [S, D])
    xown = din("xown", [T, D])
    xhalo = din("xhalo", [NOWN * 128, D])
    cT = din("cT", [128, 8])
    w_ada = din("w_ada", [D, 6 * D])
    b_adaT = din("b_adaT", [128, 48])
    w_in = din("w_in", [D, 2304])
    sinks = din("sinks", [1, 8])
    gnT = din("gnT", [128, 8])
    w_out = din("w_out", [D, D])
    lnp = din("lnp", [4, D])
    w_q = din("w_q", [D, 2048])
    subk = din("subk", [16, 128, 128])
    u_in = din("u", [NE, D])
    v_in = din("v", [NE, D])
    c_ident = din("c_ident", [128, 128])
    c_ntri = din("c_ntri", [128, 128])
    c_iota = din("c_iota", [128, 128])
    c_bm = din("c_bm", [128, 8])
    c_sbmask = din("c_sbmask", [128, 2 * 4 * 512])
    c_swab = din("c_swab", [128, 2 * 8 * 256])
    out = nc.dram_tensor("out", [T, D], F32, kind="ExternalOutput").ap()

    def scratch(name, shape, dt):
        return nc.dram_tensor(name, shape, dt, kind="Internal").ap()

    HTall = scratch("HTall", [8, 128, S], BF16)
    HTown = scratch("HTown", [8, 128, T], BF16)
    HThalo = scratch("HThalo", [8, 128, NOWN * 128], BF16)
    OTs = scratch("OTs", [8, 128, T], BF16)
    X1 = scratch("X1", [T, D], F32)
    H2T = scratch("H2T", [8, 128, T], BF16)
    IDX1T = scratch("IDX1T", [128, T], F32)
    IDX2T = scratch("IDX2T", [128, T], F32)
    GT = scratch("GT", [128, 16, T], F32)
    UT = scratch("UT", [8, 128, NE], BF16)
    VB = scratch("VB", [NE, D], BF16)
    DB = {n: Buf(n) for n in ["HTall", "HTown", "HThalo", "OTs", "X1", "H2T", "IDX1T", "IDX2T", "GT", "UT", "VB", "out"]}

    AW = 52800
    arena = nc.alloc_sbuf_tensor("arena", [128, AW], F32)
    st = {"off": 0, "n": 0}

    def alloc(free_shape, dt=F32, name=None):
        n = int(np.prod(free_shape))
        words = n if dt in (F32, U32) else (n + 1) // 2
        words = (words + 7) // 8 * 8
        off = st["off"]
        st["off"] += words
        assert st["off"] <= AW, ("SBUF arena overflow", st["off"])
        a = arena[:, off:off + words]
        if dt != F32:
            a = a.bitcast(dt)
        a = a[:, 0:n]
        if len(free_shape) == 2:
            a = a.rearrange("p (a b) -> p a b", a=free_shape[0])
        elif len(free_shape) == 3:
            a = a.rearrange("p (a b c) -> p a b c", a=free_shape[0], b=free_shape[1])
        st["n"] += 1
        return a, Buf(name or "b%d" % st["n"])

    class Rot:
        def __init__(self, n, shape, dt=F32):
            self.items = [alloc(shape, dt) for _ in range(n)]
            self.i = 0

        def next(self):
            it = self.items[self.i % len(self.items)]
            self.i += 1
            return it

    ps = [nc.alloc_psum_tensor("ps%d" % i, [128, 512], F32) for i in range(8)]
    PB = [Buf("ps%d" % i) for i in range(8)]
    pst = {"i": 0}

    def bank():
        i = pst["i"] % 8
        pst["i"] += 1
        return ps[i][:, :], PB[i]

    def mm_group(outs, reads, writes):
        def fn(e, outs=outs):
            r = None
            for (o, l, rr, s0, s1) in outs:
                r = e.matmul(o, lhsT=l, rhs=rr, start=s0, stop=s1)
            return r
        P.op("pe", fn, reads=reads, writes=writes)

    def act(out_ap, in_ap, func, reads, writes, bias=0.0, scale=1.0, accum=None):
        def fn(e):
            if accum is not None:
                return e.activation(out=out_ap, in_=in_ap, func=func, bias=bias, scale=scale, accum_out=accum)
            return e.activation(out=out_ap, in_=in_ap, func=func, bias=bias, scale=scale)
        P.op("act", fn, reads=reads, writes=writes)

    def tt(eng, out_ap, a, b, op, reads, writes):
        P.op(eng, lambda e: e.tensor_tensor(out=out_ap, in0=a, in1=b, op=op), reads=reads, writes=writes)

    def ts(eng, out_ap, a, s1, s2, op0, op1, reads, writes, accum=None):
        def fn(e):
            if accum is not None:
                return e.tensor_scalar(out=out_ap, in0=a, scalar1=s1, scalar2=s2, op0=op0, op1=op1, accum_out=accum)
            if op1 is None:
                return e.tensor_scalar(out=out_ap, in0=a, scalar1=s1, scalar2=None, op0=op0)
            return e.tensor_scalar(out=out_ap, in0=a, scalar1=s1, scalar2=s2, op0=op0, op1=op1)
        P.op(eng, fn, reads=reads, writes=writes)

    def cp(eng, out_ap, in_ap, reads, writes):
        if eng == "act":
            P.op("act", lambda e: e.copy(out=out_ap, in_=in_ap), reads=reads, writes=writes)
        else:
            P.op(eng, lambda e: e.tensor_copy(out=out_ap, in_=in_ap), reads=reads, writes=writes)

    def load(eng, dst, src, buf, extra_reads=()):
        P.dma(eng, dst, src, buf, reads=list(extra_reads), writes=[buf])

    def store(dst, src, buf, dbuf):
        P.dma("sp", dst, src, buf, reads=[buf], writes=[dbuf])

    ident, b_ident = alloc([128])
    identb, b_identb = alloc([128], BF16)
    onesf, b_onesf = alloc([128])
    onesb, b_onesb = alloc([128], BF16)
    ntri, b_ntri = alloc([128], BF16)
    nones, b_nones = alloc([128], BF16)
    iota, b_iota = alloc([128])
    bm, b_bm = alloc([8])
    ada, b_ada = alloc([48])
    s1p1, b_s1p1 = alloc([8])
    s2p1c, b_s2p1c = alloc([8])
    gate1_bc, b_g1 = alloc([D])
    s2p1_bc, b_s2 = alloc([D])
    shift2_bc, b_sh2 = alloc([D])
    gate2_bc, b_g2 = alloc([D])
    lnp_bc, b_lnp = alloc([4, D])
    sink_bc, b_sink = alloc([8])
    dummy, b_dummy = alloc([8])

    load("sp", ident, c_ident, b_ident)
    load("pool", identb, c_ident, b_identb)
    load("pool", ntri, c_ntri, b_ntri)
    load("sp", iota, c_iota, b_iota)
    load("sp", bm, c_bm, b_bm)
    P.op("pool", lambda e: e.memset(onesf, 1.0), writes=[b_onesf])
    P.op("pool", lambda e: e.memset(onesb, 1.0), writes=[b_onesb])
    P.op("pool", lambda e: e.memset(nones, -1.0), writes=[b_nones])
    load("sp", lnp_bc, bass.AP(tensor=lnp.tensor, offset=0, ap=[[0, 128], [D, 4], [1, D]]), b_lnp)
    load("sp", sink_bc, bass.AP(tensor=sinks.tensor, offset=0, ap=[[0, 128], [1, 8]]), b_sink)
    PERM_END = st["off"]

    sc_, b_sc = alloc([8])
    csb, b_csb = alloc([8])
    bada, b_bada = alloc([48])
    load("sp", csb, cT, b_csb)
    load("sp", bada, b_adaT, b_bada)
    act(sc_, csb, AF.Silu, [b_csb], [b_sc])
    wa = Rot(2, [8, 1024])
    pa, bpa = bank()
    for gi in range(6):
        wt, bwt = wa.next()
        load("sp", wt, w_ada[:, gi * 1024:(gi + 1) * 1024].rearrange("(c p) n -> p c n", p=128), bwt)
        for jj in range(8):
            j = gi * 8 + jj
            mm_group([(pa[:, j:j + 1], wt[:, dc, jj * 128:(jj + 1) * 128], sc_[:, dc:dc + 1], dc == 0, dc == 7) for dc in range(8)],
                     reads=[bwt, b_sc], writes=[bpa])
    tt("dve", ada, pa[:, 0:48], bada, ALU.add, [bpa, b_bada], [b_ada])
    ts("dve", s1p1, ada[:, 8:16], 1.0, None, ALU.add, None, [b_ada], [b_s1p1])
    ts("dve", s2p1c, ada[:, 32:40], 1.0, None, ALU.add, None, [b_ada], [b_s2p1c])
    shift1 = ada[:, 0:8]
    diag = Rot(2, [128])

    def bcast_row(dst, bdst, col, bcol):
        for half in range(2):
            pb_, bpb_ = bank()
            for q in range(4):
                dc = half * 4 + q
                dg, bdg = diag.next()
                ts("dve", dg, ident, col[:, dc:dc + 1], None, ALU.mult, None, [b_ident, bcol], [bdg])
                mm_group([(pb_[:, q * 128:(q + 1) * 128], onesf, dg, True, True)], reads=[b_onesf, bdg], writes=[bpb_])
            cp("act", dst[:, half * 512:(half + 1) * 512], pb_, [bpb_], [bdst])

    bcast_row(gate1_bc, b_g1, ada[:, 16:24], b_ada)
    bcast_row(s2p1_bc, b_s2, s2p1c, b_s2p1c)
    bcast_row(shift2_bc, b_sh2, ada[:, 24:32], b_ada)
    bcast_row(gate2_bc, b_g2, ada[:, 40:48], b_ada)

    xt = Rot(3, [D])
    hTt = Rot(2, [8, 512], BF16)
    hT_aux = {}

    def make_hT(src, ntok, dst, bdst):
        for t0 in range(0, ntok, 512):
            n = min(512, ntok - t0)
            ht, bht = hTt.next()
            bht2 = hT_aux.setdefault(id(bht), Buf("ht_aux"))
            for s in range(n // 128):
                x_, bx = xt.next()
                load("sp", x_, src[t0 + s * 128:t0 + (s + 1) * 128, :], bx)
                for half in range(2):
                    pb_, bpb_ = bank()
                    P.op("pe", lambda e, pb_=pb_, x_=x_, half=half: [e.transpose(out=pb_[:, q * 128:(q + 1) * 128], in_=x_[:, (half * 4 + q) * 128:(half * 4 + q + 1) * 128], identity=ident) for q in range(4)][-1],
                         reads=[bx, b_ident], writes=[bpb_])
                    for q in range(4):
                        dc = half * 4 + q
                        if True:
                            act(ht[:, dc, s * 128:(s + 1) * 128], pb_[:, q * 128:(q + 1) * 128], AF.Identity, [bpb_, b_s1p1, b_ada], [bht],
                                bias=shift1[:, dc:dc + 1], scale=s1p1[:, dc:dc + 1])
                        else:
                            ts("dve", ht[:, dc, s * 128:(s + 1) * 128], pb_[:, q * 128:(q + 1) * 128], s1p1[:, dc:dc + 1], shift1[:, dc:dc + 1], ALU.mult, ALU.add,
                               [bpb_, b_s1p1, b_ada], [bht2])
            P.dma("sp", dst[:, :, t0:t0 + n].rearrange("c p t -> p c t"), ht[:, :, 0:n], bht, reads=[bht, bht2], writes=[bdst])

    make_hT(xall, S, HTall, DB["HTall"])
    make_hT(xown, T, HTown, DB["HTown"])
    make_hT(xhalo, NOWN * 128, HThalo, DB["HThalo"])
    P.barrier()
    st["off"] = PERM_END

    def wload(cols0, ncols, name):
        w, bw = alloc([8, ncols], BF16, name)
        load("pool", w, w_in[:, cols0:cols0 + ncols].rearrange("(c p) n -> p c n", p=128), bw)
        return w, bw

    def proj_fm(dst, bdst, w, bw, wc0, hT, bhT, t0, n, scale=1.0, eng="act"):
        pb_, bpb_ = bank()
        mm_group([(pb_[:, 0:n], w[:, dc, wc0:wc0 + 128], hT[:, dc, t0:t0 + n], dc == 0, dc == 7) for dc in range(8)],
                 reads=[bw, bhT], writes=[bpb_])
        if eng == "act":
            act(dst, pb_[:, 0:n], AF.Copy, [bpb_], [bdst], scale=scale)
        else:
            cp(eng, dst, pb_[:, 0:n], [bpb_], [bdst])

    wqa, bwqa = wload(0, 512, "wqa")
    wkd, bwkd = alloc([8, 256], BF16, "wkd")
    wvd, bwvd = alloc([8, 256], BF16, "wvd")
    for g in range(2):
        for dup in range(2):
            load("pool", wkd[:, :, (g * 2 + dup) * 64:(g * 2 + dup + 1) * 64], w_in[:, 512 + g * 64:512 + (g + 1) * 64].rearrange("(c p) n -> p c n", p=128), bwkd)
            load("pool", wvd[:, :, (g * 2 + dup) * 64:(g * 2 + dup + 1) * 64], w_in[:, 640 + g * 64:640 + (g + 1) * 64].rearrange("(c p) n -> p c n", p=128), bwvd)
    swab, b_swab = alloc([2, 8, 256], F32, "swab")
    load("sp", swab, c_swab.rearrange("p (a h k) -> p a h k", a=2, h=8), b_swab)
    hTc, bhTc_h = alloc([8, 640], BF16, "hTc")
    bhTc_o = Buf("hTc_o")
    qaT, bqaT = alloc([4, 512], BF16)
    kaT, bkaT = alloc([2, 640], BF16)
    vad, bvad = alloc([5, 256], BF16)
    oTa, boTa = alloc([4, 512], BF16)
    r_sc1 = Rot(8, [256])
    r_p = Rot(8, [256])
    r_pn = Rot(8, [256], BF16)
    r_pT = Rot(8, [2, 128], BF16)
    r_small = Rot(8, [8])
    for i in range(NOWN):
        load("sp", hTc[:, :, 0:128], HThalo[:, :, i * 128:(i + 1) * 128].rearrange("c p t -> p c t"), bhTc_h, extra_reads=[DB["HThalo"]])
        load("sp", hTc[:, :, 128:640], HTown[:, :, i * 512:(i + 1) * 512].rearrange("c p t -> p c t"), bhTc_o, extra_reads=[DB["HTown"]])
        bh = Buf("hTc_both")
        for j in range(4):
            pb_, bpb_ = bank()
            mm_group([(pb_, wqa[:, dc, j * 128:(j + 1) * 128], hTc[:, dc, 128:640], dc == 0, dc == 7) for dc in range(8)],
                     reads=[bwqa, bhTc_o], writes=[bpb_])
            act(qaT[:, j, :], pb_, AF.Copy, [bpb_], [bqaT], scale=0.125)
        for g in range(2):
            for ch in range(2):
                pb_, bpb_ = bank()
                mm_group([(pb_[:, 0:320], wkd[:, dc, g * 128:(g + 1) * 128], hTc[:, dc, ch * 320:(ch + 1) * 320], dc == 0, dc == 7) for dc in range(8)],
                         reads=[bwkd, bhTc_o, bhTc_h], writes=[bpb_])
                cp("dve", kaT[:, g, ch * 320:(ch + 1) * 320], pb_[:, 0:320], [bpb_], [bkaT])
        for blk in range(5):
            pb_, bpb_ = bank()
            mm_group([(pb_[:, 0:256], hTc[:, dc, blk * 128:(blk + 1) * 128], wvd[:, dc, :], dc == 0, dc == 7) for dc in range(8)],
                     reads=[bwvd, bhTc_o, bhTc_h], writes=[bpb_])
            cp("dve", vad[:, blk, :], pb_[:, 0:256], [bpb_], [bvad])
        for r in range(4):
            var = 0 if (i == 0 and r == 0) else 1
            H8 = range(8)
            hb_ = [(h % 2) * 64 for h in H8]
            pr_ = [h // 2 for h in H8]
            g_ = [h // 4 for h in H8]
            pbs = []
            for h in H8:
                pb_, bpb_ = bank()
                mm_group([(pb_[:, 0:256], qaT[hb_[h]:hb_[h] + 64, pr_[h], r * 128:(r + 1) * 128], kaT[hb_[h]:hb_[h] + 64, g_[h], r * 128:r * 128 + 256], True, True)],
                         reads=[bqaT, bkaT], writes=[bpb_])
                pbs.append((pb_, bpb_))
            scs = [r_sc1.next() for h in H8]
            sms = [r_small.next() for h in H8]
            pps = [r_p.next() for h in H8]
            pns = [r_pn.next() for h in H8]
            pTs = [r_pT.next() for h in H8]
            for h in H8:
                tt("dve", scs[h][0], pbs[h][0][:, 0:256], swab[:, var, h, :], ALU.add, [pbs[h][1], b_swab], [scs[h][1]])
            for h in H8:
                P.op("dve", lambda e, sm=sms[h][0], sc1=scs[h][0]: e.reduce_max(out=sm[:, 0:1], in_=sc1, axis=AX.X), reads=[scs[h][1]], writes=[sms[h][1]])
            for h in H8:
                sm, bsm = sms[h]
                ts("dve", sm[:, 1:2], sm[:, 0:1], sink_bc[:, h:h + 1], -1.0, ALU.max, ALU.mult, [bsm, b_sink], [bsm])
            for h in H8:
                sm, bsm = sms[h]
                act(pps[h][0], scs[h][0], AF.Exp, [scs[h][1], bsm], [pps[h][1], bsm], bias=sm[:, 1:2], accum=sm[:, 2:3])
            for h in H8:
                sm, bsm = sms[h]
                act(sm[:, 3:4], sink_bc[:, h:h + 1], AF.Exp, [b_sink, bsm], [bsm], bias=sm[:, 1:2])
            for h in H8:
                sm, bsm = sms[h]
                tt("dve", sm[:, 4:5], sm[:, 2:3], sm[:, 3:4], ALU.add, [bsm], [bsm])
            for h in H8:
                sm, bsm = sms[h]
                P.op("dve", lambda e, sm=sm: e.reciprocal(out=sm[:, 5:6], in_=sm[:, 4:5]), reads=[bsm], writes=[bsm])
            for h in H8:
                sm, bsm = sms[h]
                ts("dve", pns[h][0], pps[h][0], sm[:, 5:6], None, ALU.mult, None, [pps[h][1], bsm], [pns[h][1]])
            pb2s = []
            for h in H8:
                pb2, bpb2 = bank()
                pb2b = pb2.bitcast(BF16)
                P.op("pe", lambda e, pb2b=pb2b, pn=pns[h][0]: [e.transpose(out=pb2b[:, kb * 128:(kb + 1) * 128], in_=pn[:, kb * 128:(kb + 1) * 128], identity=identb) for kb in range(2)][-1],
                     reads=[pns[h][1], b_identb], writes=[bpb2])
                pb2s.append((pb2b, bpb2))
            for h in H8:
                cp("act", pTs[h][0], pb2s[h][0][:, 0:256].rearrange("p (a b) -> p a b", a=2), [pb2s[h][1]], [pTs[h][1]])
            pb3s = []
            for h in H8:
                pb3, bpb3 = bank()
                mm_group([(pb3[:, 0:128], vad[:, r + kb, g_[h] * 128:(g_[h] + 1) * 128], pTs[h][0][:, kb, :], kb == 0, kb == 1) for kb in range(2)],
                         reads=[bvad, pTs[h][1]], writes=[bpb3])
                pb3s.append((pb3, bpb3))
            for h in H8:
                cp("act", oTa[hb_[h]:hb_[h] + 64, pr_[h], r * 128:(r + 1) * 128], pb3s[h][0][hb_[h]:hb_[h] + 64, 0:128], [pb3s[h][1]], [boTa])
        store(OTs[0:4, :, i * 512:(i + 1) * 512].rearrange("c p t -> p c t"), oTa, boTa, DB["OTs"])
    P.barrier()
    st["off"] = PERM_END

    sbmask, b_sbmask = alloc([2, 4, 512], F32, "sbmask")
    load("sp", sbmask, c_sbmask.rearrange("p (a k q) -> p a k q", a=2, k=4), b_sbmask)
    kT, bkT = alloc([2, S], BF16, "kT")
    vv, bvv = alloc([NKB, 256], BF16, "vv")
    A2_END = st["off"]
    for hp2 in range(2):
        st["off"] = A2_END
        wqb, bwqb = wload(768 + hp2 * 256, 256, "wqb")
        wkb, bwkb = wload(1280 + hp2 * 256, 256, "wkb")
        wvb, bwvb = wload(1792 + hp2 * 256, 256, "wvb")
        hTl = Rot(2, [8, 512], BF16)
        for tt_ in range(NSB):
            ht, bht = hTl.next()
            load("sp", ht, HTall[:, :, tt_ * 512:(tt_ + 1) * 512].rearrange("c p t -> p c t"), bht, extra_reads=[DB["HTall"]])
            for pr in range(2):
                proj_fm(kT[:, pr, tt_ * 512:(tt_ + 1) * 512], bkT, wkb, bwkb, pr * 128, ht, bht, 0, 512)
            for s in range(4):
                pb_, bpb_ = bank()
                mm_group([(pb_[:, 0:256], ht[:, dc, s * 128:(s + 1) * 128], wvb[:, dc, :], dc == 0, dc == 7) for dc in range(8)],
                         reads=[bwvb, bht], writes=[bpb_])
                cp("dve", vv[:, tt_ * 4 + s, :], pb_[:, 0:256], [bpb_], [bvv])
        qbT, bqbT = alloc([2, 512], BF16)
        oTb, boTb = alloc([2, 512], BF16)
        r_e = Rot(8, [512])
        r_sp = Rot(12, [512], BF16)
        r_a = Rot(8, [512], BF16)
        Rl = [alloc([512]) for _ in range(4)]
        Rbl = [[alloc([512], BF16) for _ in range(2)] for _ in range(4)]
        for i in range(NOWN):
            ht, bht = hTl.next()
            load("sp", ht, HTown[:, :, i * 512:(i + 1) * 512].rearrange("c p t -> p c t"), bht, extra_reads=[DB["HTown"]])
            for pr in range(2):
                proj_fm(qbT[:, pr, :], bqbT, wqb, bwqb, pr * 128, ht, bht, 0, 512, scale=0.125)
            nkb = 8 * i + 8
            for hl in range(4):
                R, bR = Rl[hl]
                P.op("pool", lambda e, R=R: e.memset(R, 0.0), writes=[bR])
                for pp_ in range(2):
                    Rb, bRb = Rbl[hl][pp_]
                    P.op("pool", lambda e, Rb=Rb: e.memset(Rb, 0.0), writes=[bRb])
            pos = [bank() for _ in range(4)]
            pobufs = [b for (_, b) in pos]

            def fbank():
                while True:
                    p_, b_ = bank()
                    if b_ not in pobufs:
                        return p_, b_

            def stage1(kb):
                masked = kb >= 8 * i
                if masked:
                    mk = sbmask[:, (kb - 8 * i) // 4, (kb - 8 * i) % 4, :]
                sps = []
                for hl in range(4):
                    pr = hl // 2
                    hb = (hl % 2) * 64
                    ksl = kT[hb:hb + 64, pr, kb * 128:(kb + 1) * 128]
                    qsl = qbT[hb:hb + 64, pr, :]
                    pz, bpz = fbank()
                    mm_group([(pz, ksl, qsl, True, True)], reads=[bkT, bqbT], writes=[bpz])
                    e_, be = r_e.next()
                    act(e_, pz, AF.Exp, [bpz], [be])
                    sp_, bsp = r_sp.next()
                    act(sp_, e_, AF.Ln, [be], [bsp], bias=1.0)
                    if masked:
                        tt("dve", sp_, sp_, mk, ALU.mult, [bsp, b_sbmask], [bsp])
                    sps.append((sp_, bsp))
                return sps

            def stage2(kb, sps):
                masked = kb >= 8 * i
                if masked:
                    mk = sbmask[:, (kb - 8 * i) // 4, (kb - 8 * i) % 4, :]
                for hl in range(4):
                    pr = hl // 2
                    hb = (hl % 2) * 64
                    ksl = kT[hb:hb + 64, pr, kb * 128:(kb + 1) * 128]
                    qsl = qbT[hb:hb + 64, pr, :]
                    sp_, bsp = sps[hl]
                    R, bR = Rl[hl]
                    Rb, bRb = Rbl[hl][kb % 2]
                    Rbn, bRbn = Rbl[hl][(kb + 1) % 2]
                    po, bpo = pos[hl]
                    if kb > 0:
                        tt("dve", Rbn, R, sp_, ALU.add, [bR, bsp], [bRbn])
                        tt("dve", R, R, sp_, ALU.add, [bR, bsp], [bR])
                    pp, bpp = fbank()
                    mm_group([(pp, ksl, qsl, True, False), (pp, ntri, sp_, False, False), (pp, nones, Rb, False, True)],
                             reads=[bkT, bqbT, b_ntri, bsp, b_nones, bRb], writes=[bpp])
                    a_, ba = r_a.next()
                    act(a_, pp, AF.Exp, [bpp], [ba])
                    if masked:
                        tt("pool", a_, a_, mk, ALU.mult, [ba, b_sbmask], [ba])
                    mm_group([(po, vv[:, kb, pr * 128:(pr + 1) * 128], a_, kb == nkb - 1, kb == 0)], reads=[bvv, ba], writes=[bpo])

            cur = stage1(nkb - 1)
            for kb in range(nkb - 1, -1, -1):
                nxt = stage1(kb - 1) if kb > 0 else None
                stage2(kb, cur)
                cur = nxt
            for hl in range(4):
                pr = hl // 2
                hb = (hl % 2) * 64
                po, bpo = pos[hl]
                cp("act", oTb[hb:hb + 64, pr, :], po[hb:hb + 64, :], [bpo], [boTb])
            store(OTs[4 + hp2 * 2:6 + hp2 * 2, :, i * 512:(i + 1) * 512].rearrange("c p t -> p c t"), oTb, boTb, DB["OTs"])
        P.barrier()
    st["off"] = PERM_END

    wos, bwos = alloc([8, D], BF16, "wos")
    gn, bgn = alloc([8], F32, "gn")
    load("sp", gn, gnT, bgn)
    wstage = Rot(2, [D])
    for ch in range(8):
        wsg, bwsg = wstage.next()
        load("sp", wsg, w_out[ch * 128:(ch + 1) * 128, :], bwsg)
        ts("dve", wos[:, ch, :], wsg, gn[:, ch:ch + 1], None, ALU.mult, None, [bwsg, bgn], [bwos])
    oTl = Rot(2, [8, 512], BF16)
    sql = Rot(2, [8, 512], BF16)
    xl = Rot(2, [D])
    t1l = Rot(2, [D])
    t2l = Rot(2, [D])
    sml = Rot(4, [16])
    stl = Rot(2, [12])

    def layer_norm_tile(y, by, dst, bdst, gi, eng2="pool"):
        stt, bst = stl.next()
        for hh in range(2):
            P.op("dve", lambda e, stt=stt, y=y, hh=hh: e.bn_stats(out=stt[:, hh * 6:(hh + 1) * 6], in_=y[:, hh * 512:(hh + 1) * 512]), reads=[by], writes=[bst])
        sm, bsm = sml.next()
        P.op("dve", lambda e, sm=sm, stt=stt: e.bn_aggr(out=sm[:, 0:2], in_=stt), reads=[bst], writes=[bsm])
        ts("dve", sm[:, 2:3], sm[:, 1:2], EPS, None, ALU.add, None, [bsm], [bsm])
        P.op("act", lambda e, sm=sm: e.sqrt(out=sm[:, 3:4], in_=sm[:, 2:3]), reads=[bsm], writes=[bsm])
        P.op("dve", lambda e, sm=sm: e.reciprocal(out=sm[:, 4:5], in_=sm[:, 3:4]), reads=[bsm], writes=[bsm])
        ts("dve", y, y, sm[:, 0:1], sm[:, 4:5], ALU.subtract, ALU.mult, [by, bsm], [by])
        tt(eng2, y, y, lnp_bc[:, gi, :], ALU.mult, [by, b_lnp], [by])
        tt(eng2, dst, y, lnp_bc[:, gi + 1, :], ALU.add, [by, b_lnp], [bdst])

    for i in range(NOWN):
        oT, boT = oTl.next()
        load("sp", oT, OTs[:, :, i * 512:(i + 1) * 512].rearrange("c p t -> p c t"), boT, extra_reads=[DB["OTs"]])
        sq, bsq = sql.next()
        tt("pool", sq, oT, oT, ALU.mult, [boT], [bsq])
        for s in range(4):
            tok = slice(s * 128, (s + 1) * 128)
            pq, bpq = bank()
            mm_group([(pq[:, grp:grp + 1], sq[:, grp * 4 + ch, tok], onesb[:, 0:1], ch == 0, ch == 3) for grp in range(2) for ch in range(4)],
                     reads=[bsq, b_onesb], writes=[bpq])
            sm, bsm = sml.next()
            ts("dve", sm[:, 0:2], pq[:, 0:2], 1.0 / 512, EPS, ALU.mult, ALU.add, [bpq], [bsm])
            P.op("act", lambda e, sm=sm: e.sqrt(out=sm[:, 2:4], in_=sm[:, 0:2]), reads=[bsm], writes=[bsm])
            P.op("dve", lambda e, sm=sm: e.reciprocal(out=sm[:, 4:6], in_=sm[:, 2:4]), reads=[bsm], writes=[bsm])
            x_, bx = xl.next()
            load("sp", x_, xown[i * 512 + s * 128:i * 512 + (s + 1) * 128, :], bx)
            t1, bt1 = t1l.next()
            t2, bt2 = t2l.next()
            for half in range(2):
                hs = slice(half * 512, (half + 1) * 512)
                pa_, bpa_ = bank()
                mm_group([(pa_, oT[:, ch, tok], wos[:, ch, hs], ch == 0, ch == 3) for ch in range(4)], reads=[boT, bwos], writes=[bpa_])
                pbb, bpbb = bank()
                mm_group([(pbb, oT[:, 4 + ch, tok], wos[:, 4 + ch, hs], ch == 0, ch == 3) for ch in range(4)], reads=[boT, bwos], writes=[bpbb])
                act(t1[:, hs], pa_, AF.Copy, [bpa_, bsm], [bt1], scale=sm[:, 4:5])
                P.op("dve", lambda e, t2=t2, pbb=pbb, sm=sm, t1=t1, hs=hs: e.scalar_tensor_tensor(out=t2[:, hs], in0=pbb, scalar=sm[:, 5:6], in1=t1[:, hs], op0=ALU.mult, op1=ALU.add),
                     reads=[bpbb, bsm, bt1], writes=[bt2])
            tt("pool", t2, t2, gate1_bc, ALU.mult, [bt2, b_g1], [bt2])
            P.op("dve", lambda e, t1=t1, x_=x_, t2=t2: e.scalar_tensor_tensor(out=t1, in0=x_, scalar=ALPHA, in1=t2, op0=ALU.mult, op1=ALU.add),
                 reads=[bx, bt2, bt1], writes=[bt1])
            layer_norm_tile(t1, bt1, t2, bt2, 0)
            store(X1[i * 512 + s * 128:i * 512 + (s + 1) * 128, :], t2, bt2, DB["X1"])
    P.barrier()
    st["off"] = PERM_END

    ul = Rot(2, [D])
    utl = Rot(2, [8, 512], BF16)
    vl = Rot(2, [4, D], BF16)
    for sc4 in range(NCH // 4):
        ut, but = utl.next()
        for cl in range(4):
            c = sc4 * 4 + cl
            u_, bu = ul.next()
            load("sp", u_, u_in[c * 128:(c + 1) * 128, :], bu)
            for half in range(2):
                pb_, bpb_ = bank()
                P.op("pe", lambda e, pb_=pb_, u_=u_, half=half: [e.transpose(out=pb_[:, q * 128:(q + 1) * 128], in_=u_[:, (half * 4 + q) * 128:(half * 4 + q + 1) * 128], identity=ident) for q in range(4)][-1],
                     reads=[bu, b_ident], writes=[bpb_])
                cp("act" if half == 0 else "dve", ut[:, half * 4:(half + 1) * 4, cl * 128:(cl + 1) * 128], pb_.rearrange("p (a b) -> p a b", a=4), [bpb_], [but])
        store(UT[:, :, sc4 * 512:(sc4 + 1) * 512].rearrange("c p e -> p c e"), ut, but, DB["UT"])
        v_, bv = vl.next()
        load("pool", v_, v_in[sc4 * 512:(sc4 + 1) * 512, :].rearrange("(cl p) d -> p cl d", p=128), bv)
        store(VB[sc4 * 512:(sc4 + 1) * 512, :].rearrange("(cl p) d -> p cl d", p=128), v_, bv, DB["VB"])
    P.barrier()
    st["off"] = PERM_END

    wq, bwq = alloc([8, 2048], BF16, "wq")
    for ch in range(4):
        load("pool", wq[:, :, ch * 512:(ch + 1) * 512], w_q[:, ch * 512:(ch + 1) * 512].rearrange("(c p) n -> p c n", p=128), bwq)
    skT, bskT = alloc([16, 128], BF16, "skT")
    skl = Rot(2, [128])
    for j in range(0, 16, 4):
        pb_, bpb_ = bank()
        for q in range(4):
            sk, bsk = skl.next()
            load("sp", sk, subk[j + q], bsk)
            P.op("pe", lambda e, pb_=pb_, sk=sk, q=q: e.transpose(out=pb_[:, q * 128:(q + 1) * 128], in_=sk, identity=ident), reads=[bsk, b_ident], writes=[bpb_])
        cp("act", skT[:, j:j + 4, :], pb_.rearrange("p (a b) -> p a b", a=4), [bpb_], [bskT])
    x1l = Rot(2, [D])
    h2l = Rot(2, [D])
    h2Tl = Rot(2, [8, 128], BF16)
    qTl = Rot(2, [16, 128], BF16)
    scl = Rot(2, [16, 128])
    wkl = Rot(16, [128])
    tvl = Rot(2, [16, 16])
    til = Rot(2, [16, 16], U32)
    tifl = Rot(2, [2, 128])
    candl = Rot(2, [8, 256])
    cwl = Rot(8, [256])
    el = Rot(2, [8, 256])
    gl_ = Rot(2, [8, 256])
    c16l = Rot(2, [8, 16])
    zl = Rot(2, [24])
    i1Tl = Rot(2, [128])
    i2Tl = Rot(2, [128])
    gTl = Rot(2, [16, 128])
    for ti_ in range(T // 128):
        tsl = slice(ti_ * 128, (ti_ + 1) * 128)
        x1, bx1 = x1l.next()
        load("sp", x1, X1[tsl, :], bx1, extra_reads=[DB["X1"]])
        h2, bh2 = h2l.next()
        tt("pool", h2, x1, s2p1_bc, ALU.mult, [bx1, b_s2], [bh2])
        tt("pool", h2, h2, shift2_bc, ALU.add, [bh2, b_sh2], [bh2])
        h2T, bh2T = h2Tl.next()
        for half in range(2):
            pb_, bpb_ = bank()
            P.op("pe", lambda e, pb_=pb_, h2=h2, half=half: [e.transpose(out=pb_[:, q * 128:(q + 1) * 128], in_=h2[:, (half * 4 + q) * 128:(half * 4 + q + 1) * 128], identity=ident) for q in range(4)][-1],
                 reads=[bh2, b_ident], writes=[bpb_])
            cp("act", h2T[:, half * 4:(half + 1) * 4, :], pb_.rearrange("p (a b) -> p a b", a=4), [bpb_], [bh2T])
        store(H2T[:, :, tsl].rearrange("c p t -> p c t"), h2T, bh2T, DB["H2T"])
        qT, bqT = qTl.next()
        for j4 in range(4):
            pb_, bpb_ = bank()
            mm_group([(pb_[:, q * 128:(q + 1) * 128], wq[:, dc, (j4 * 4 + q) * 128:(j4 * 4 + q + 1) * 128], h2T[:, dc, :], dc == 0, dc == 7) for q in range(4) for dc in range(8)],
                     reads=[bwq, bh2T], writes=[bpb_])
            cp("act", qT[:, j4 * 4:(j4 + 1) * 4, :], pb_.rearrange("p (a b) -> p a b", a=4), [bpb_], [bqT])
        sc, bsc = scl.next()
        for j4 in range(4):
            pb_, bpb_ = bank()
            mm_group([(pb_[:, q * 128:(q + 1) * 128], qT[:, j4 * 4 + q, :], skT[:, j4 * 4 + q, :], True, True) for q in range(4)],
                     reads=[bqT, bskT], writes=[bpb_])
            cp("dve" if j4 % 2 else "act", sc[:, j4 * 4:(j4 + 1) * 4, :], pb_.rearrange("p (a b) -> p a b", a=4), [bpb_], [bsc])
        tv, btv = tvl.next()
        tiu, btiu = til.next()
        tif, btif = tifl.next()
        J = range(16)
        tvb = [Buf("tv%d" % j) for j in J]
        tib = [Buf("ti%d" % j) for j in J]
        wks = [wkl.next() for j in J]
        P.op("dve", lambda e: e.memset(dummy[:, 4:5], 0.0), reads=[], writes=[btv, btiu])
        for j in J:
            P.op("dve", lambda e, tv=tv, sc=sc, j=j: e.max(out=tv[:, j, 0:8], in_=sc[:, j, :]), reads=[bsc, btv], writes=[tvb[j]])
        for j in J:
            P.op("dve", lambda e, tv=tv, sc=sc, j=j, wk=wks[j][0]: e.match_replace(out=wk, in_to_replace=tv[:, j, 0:8], in_values=sc[:, j, :], imm_value=-1e30), reads=[bsc, tvb[j]], writes=[wks[j][1]])
        for j in J:
            P.op("dve", lambda e, tv=tv, j=j, wk=wks[j][0]: e.max(out=tv[:, j, 8:16], in_=wk), reads=[wks[j][1], tvb[j]], writes=[tvb[j]])
        for j in J:
            P.op("dve", lambda e, tv=tv, sc=sc, j=j, tiu=tiu: e.max_index(out=tiu[:, j, 0:8], in_max=tv[:, j, 0:8], in_values=sc[:, j, :]), reads=[bsc, tvb[j], btiu], writes=[tib[j]])
        for j in J:
            P.op("dve", lambda e, tv=tv, sc=sc, j=j, tiu=tiu: e.max_index(out=tiu[:, j, 8:16], in_max=tv[:, j, 8:16], in_values=sc[:, j, :]), reads=[bsc, tvb[j], tib[j]], writes=[tib[j]])
        P.op("dve", lambda e: e.memset(dummy[:, 0:1], 0.0), reads=tvb, writes=[btv])
        P.op("dve", lambda e: e.memset(dummy[:, 1:2], 0.0), reads=tib, writes=[btiu])
        for p_ in range(2):
            cp("pool", V(tif, p_ * 128, [[16, 8], [1, 16]]), V(tiu, p_ * 16, [[32, 8], [1, 16]]), [btiu], [btif])
        cand, bcand = candl.next()
        tt("dve", V(cand, 0, [[256, 8], [16, 16], [1, 16]]), V(tv, 0, [[32, 8], [1, 16], [0, 16]]), V(tv, 16, [[32, 8], [0, 16], [1, 16]]), ALU.add,
           [btv], [bcand])
        c16, bc16 = c16l.next()
        z_, bz = zl.next()
        ee, bee = el.next()
        gg, bgg = gl_.next()
        H8 = range(8)
        cws = [cwl.next() for h in H8]
        c16b = [Buf("c16_%d" % h) for h in H8]
        zb = [Buf("z_%d" % h) for h in H8]
        eeb = [Buf("ee_%d" % h) for h in H8]
        ggb = [Buf("gg_%d" % h) for h in H8]
        P.op("dve", lambda e: e.memset(dummy[:, 5:6], 0.0), reads=[], writes=[bc16, bz, bee, bgg])
        for h in H8:
            P.op("dve", lambda e, c16=c16, cand=cand, h=h: e.max(out=c16[:, h, 0:8], in_=cand[:, h, :]), reads=[bcand, bc16], writes=[c16b[h]])
        for h in H8:
            P.op("dve", lambda e, c16=c16, cand=cand, h=h, cw=cws[h][0]: e.match_replace(out=cw, in_to_replace=c16[:, h, 0:8], in_values=cand[:, h, :], imm_value=-1e30), reads=[bcand, c16b[h]], writes=[cws[h][1]])
        for h in H8:
            P.op("dve", lambda e, c16=c16, h=h, cw=cws[h][0]: e.max(out=c16[:, h, 8:16], in_=cw), reads=[cws[h][1], c16b[h]], writes=[c16b[h]])
        for h in H8:
            ts("dve", z_[:, h:h + 1], c16[:, h, 0:1], -1.0, None, ALU.mult, None, [c16b[h], bz], [zb[h]])
        for h in H8:
            act(ee[:, h, :], cand[:, h, :], AF.Exp, [bcand, zb[h], bee], [eeb[h]], bias=z_[:, h:h + 1])
        for h in H8:
            P.op("dve", lambda e, gg=gg, cand=cand, c16=c16, ee=ee, z_=z_, h=h: e.scalar_tensor_tensor(out=gg[:, h, :], in0=cand[:, h, :], scalar=c16[:, h, 15:16], in1=ee[:, h, :], op0=ALU.is_ge, op1=ALU.mult, accum_out=z_[:, 8 + h:9 + h]),
                 reads=[bcand, c16b[h], eeb[h], zb[h], bgg], writes=[ggb[h], zb[h]])
        P.op("dve", lambda e: e.memset(dummy[:, 2:3], 0.0), reads=zb + c16b, writes=[bz, bc16])
        P.op("dve", lambda e: e.memset(dummy[:, 3:4], 0.0), reads=ggb + eeb, writes=[bgg, bee])
        P.op("dve", lambda e, z_=z_: e.reciprocal(out=z_[:, 16:24], in_=z_[:, 8:16]), reads=[bz], writes=[bz])
        tt("dve", gg, gg, V(z_, 16, [[1, 8], [0, 256]]), ALU.mult, [bgg, bz], [bgg])
        i1T, bi1T = i1Tl.next()
        i2T, bi2T = i2Tl.next()
        pb_, bpb_ = bank()
        P.op("pe", lambda e, pb_=pb_, tif=tif: [e.transpose(out=pb_[:, p_ * 128:(p_ + 1) * 128], in_=tif[:, p_, :], identity=ident) for p_ in range(2)][-1],
             reads=[btif, b_ident], writes=[bpb_])
        cp("act", i1T, pb_[:, 0:128], [bpb_], [bi1T])
        cp("dve", i2T, pb_[:, 128:256], [bpb_], [bi2T])
        store(IDX1T[:, tsl], i1T, bi1T, DB["IDX1T"])
        store(IDX2T[:, tsl], i2T, bi2T, DB["IDX2T"])
        gT, bgT = gTl.next()
        for b4 in range(4):
            pb_, bpb_ = bank()
            P.op("pe", lambda e, pb_=pb_, gg=gg, b4=b4: [e.transpose(out=pb_[:, q * 128:(q + 1) * 128], in_=V(gg, b4 * 4 + q, [[16, 128]]), identity=ident) for q in range(4)][-1],
                 reads=[bgg, b_ident], writes=[bpb_])
            cp("act" if b4 % 2 else "dve", gT[:, b4 * 4:(b4 + 1) * 4, :], pb_.rearrange("p (a b) -> p a b", a=4), [bpb_], [bgT])
        store(GT[:, :, tsl], gT, bgT, DB["GT"])
    P.barrier()
    st["off"] = PERM_END

    Gs, bGs = alloc([256, 128], BF16, "Gs")
    h2g, bh2g = alloc([8, 256], BF16, "h2g")
    i1g, bi1g = alloc([256], F32)
    i2g, bi2g = alloc([256], F32)
    gTg, bgTg = alloc([16, 256], F32)
    TS = 16
    oh1l = Rot(2, [TS, 128], BF16)
    oh2l = Rot(2, [TS, 128], BF16)
    gbl = Rot(2, [TS, 128], BF16)
    apl = Rot(3, [4, 128], BF16)
    utl = Rot(2, [8, 512], BF16)
    vl = Rot(3, [4, D], BF16)
    gel = Rot(4, [256], BF16)
    wl = Rot(4, [256], BF16)
    x1l = Rot(1, [D])
    t1l = Rot(1, [D])
    t2l = Rot(1, [D])
    for gi in range(T // 256):
        g0 = gi * 256
        load("sp", h2g, H2T[:, :, g0:g0 + 256].rearrange("c p t -> p c t"), bh2g, extra_reads=[DB["H2T"]])
        load("sp", i1g, IDX1T[:, g0:g0 + 256], bi1g, extra_reads=[DB["IDX1T"]])
        load("sp", i2g, IDX2T[:, g0:g0 + 256], bi2g, extra_reads=[DB["IDX2T"]])
        load("sp", gTg, GT[:, :, g0:g0 + 256], bgTg, extra_reads=[DB["GT"]])
        gprev = None

        def g_stage2(oh2, boh2, ap_, bap, tok0, par):
            pb2, bpb2 = bank()
            mm_group([(pb2[:, q * 128:(q + 1) * 128], oh2[:, (tok0 % TS) + q, :], ap_[:, q, :], True, True) for q in range(4)],
                     reads=[boh2, bap], writes=[bpb2])
            cp("dve" if par % 2 else "act", Gs[:, tok0:tok0 + 4, :], pb2.rearrange("p (a b) -> p a b", a=4), [bpb2], [bGs])

        for sub in range(256 // TS):
            t0 = sub * TS
            oh1, boh1 = oh1l.next()
            oh2, boh2 = oh2l.next()
            gb, bgb = gbl.next()
            tt("dve", oh1, V(iota, 0, [[0, TS], [1, 128]]), V(i1g, t0, [[1, TS], [0, 128]]), ALU.is_equal, [b_iota, bi1g], [boh1])
            tt("dve", oh2, V(iota, 0, [[0, TS], [1, 128]]), V(i2g, t0, [[1, TS], [0, 128]]), ALU.is_equal, [b_iota, bi2g], [boh2])
            tt("pool", V(gb, 0, [[128, TS], [16, 8], [1, 16]]), V(gTg, t0, [[1, TS], [0, 8], [256, 16]]), V(bm, 0, [[0, TS], [1, 8], [0, 16]]), ALU.mult,
               [bgTg, b_bm], [bgb])
            for t4 in range(TS // 4):
                pb_, bpb_ = bank()
                mm_group([(pb_[:, q * 128:(q + 1) * 128], gb[:, t4 * 4 + q, :], oh1[:, t4 * 4 + q, :], True, True) for q in range(4)],
                         reads=[bgb, boh1], writes=[bpb_])
                ap_, bap = apl.next()
                cp("act", ap_, pb_.rearrange("p (a b) -> p a b", a=4), [bpb_], [bap])
                if gprev is not None:
                    g_stage2(*gprev)
                gprev = (oh2, boh2, ap_, bap, t0 + t4 * 4, t4)
        g_stage2(*gprev)
        gprev = None
        ybanks = [bank() for _ in range(4)]
        ybufs = [b for (_, b) in ybanks]
        pend = []

        def y_stage(w_, bw_, v_, bv, cl, c):
            mm_group([(ybanks[tl * 2 + half][0], w_[:, tl * 128:(tl + 1) * 128], v_[:, cl, half * 512:(half + 1) * 512], c == 0, c == NCH - 1)
                      for tl in range(2) for half in range(2)], reads=[bw_, bv], writes=ybufs)

        for sc4 in range(NCH // 4):
            ut, but = utl.next()
            load("sp", ut, UT[:, :, sc4 * 512:(sc4 + 1) * 512].rearrange("c p e -> p c e"), but, extra_reads=[DB["UT"]])
            v_, bv = vl.next()
            load("sp", v_, VB[sc4 * 512:(sc4 + 1) * 512, :].rearrange("(cl p) d -> p cl d", p=128), bv, extra_reads=[DB["VB"]])
            for cl in range(4):
                c = sc4 * 4 + cl
                ph, bph = bank()
                while bph in ybufs:
                    ph, bph = bank()
                mm_group([(ph[:, 0:256], ut[:, dc, cl * 128:(cl + 1) * 128], h2g[:, dc, :], dc == 0, dc == 7) for dc in range(8)],
                         reads=[but, bh2g], writes=[bph])
                ge, bge = gel.next()
                act(ge, ph[:, 0:256], AF.Gelu, [bph], [bge])
                w_, bw_ = wl.next()
                tt("dve", w_, ge, V(Gs, c, [[128, 256]]), ALU.mult, [bge, bGs], [bw_])
                pend.append((w_, bw_, v_, bv, cl, c))
                if len(pend) > 2:
                    y_stage(*pend.pop(0))
        while pend:
            y_stage(*pend.pop(0))
        for tl in range(2):
            tsl = slice(g0 + tl * 128, g0 + (tl + 1) * 128)
            x1, bx1 = x1l.next()
            load("sp", x1, X1[tsl, :], bx1, extra_reads=[DB["X1"]])
            t1, bt1 = t1l.next()
            t2, bt2 = t2l.next()
            for half in range(2):
                hs = slice(half * 512, (half + 1) * 512)
                yb, byb = ybanks[tl * 2 + half]
                tt("dve", t1[:, hs], yb, gate2_bc[:, hs], ALU.mult, [byb, b_g2], [bt1])
            P.op("dve", lambda e, t1=t1, x1=x1: e.scalar_tensor_tensor(out=t1, in0=x1, scalar=ALPHA, in1=t1, op0=ALU.mult, op1=ALU.add),
                 reads=[bx1, bt1], writes=[bt1])
            layer_norm_tile(t1, bt1, t2, bt2, 2)
            store(out[tsl, :], t2, bt2, DB["out"])
    with nc.allow_low_precision("bf16 matmuls (reference tolerance calibrated for bf16)"):
        P.emit(nc)
    return nc


def host_consts(parity):
    p = np.arange(128)
    c = {}
    c["c_ident"] = np.eye(128, dtype=np.float32)
    c["c_ntri"] = -(p[:, None] >= p[None, :]).astype(np.float32)
    c["c_iota"] = np.tile(np.arange(128, dtype=np.float32)[None, :], (128, 1))
    c["c_bm"] = (p[:, None] // 16 == np.arange(8)[None, :]).astype(np.float32)
    m = np.zeros((128, 2, 4, 512), np.float32)
    q = np.arange(512)
    for a in range(2):
        for kb in range(4):
            kpos = a * 512 + kb * 128 + p
            qpos = parity * 512 + q
            m[:, a, kb, :] = (kpos[:, None] < qpos[None, :]).astype(np.float32)
    c["c_sbmask"] = m.reshape(128, -1)
    slopes = np.array([2.0 ** (-8.0 * (i + 1) / 8) for i in range(8)], np.float32)
    kj = np.arange(256)
    dist = (128 + p[:, None] - kj[None, :]).astype(np.float32)
    valid = (dist >= 0) & (dist < 128)
    sw = np.zeros((128, 2, 8, 256), np.float32)
    for var in range(2):
        vv = valid.copy()
        if var == 0 and parity == 0:
            vv = vv & (kj >= 128)[None, :]
        for h in range(8):
            sw[:, var, h, :] = np.where(vv, -slopes[h] * dist, NEG)
    c["c_swab"] = sw.reshape(128, -1)
    return c


def make_in_maps(inp, NSB):
    S = 512 * NSB
    x = np.asarray(inp["x"], np.float32)
    Bn = x.shape[0]
    f = lambda a: np.ascontiguousarray(np.asarray(a, np.float32))
    common = {
        "w_ada": f(inp["w_ada"][0]), "b_adaT": f(np.asarray(inp["b_ada"][0]).reshape(48, 128).T),
        "w_in": f(inp["w_in"][0]), "sinks": f(np.asarray(inp["swa_sinks"][0]).reshape(1, 8)),
        "gnT": f(np.concatenate([np.asarray(inp["group_norm_a"][0]), np.asarray(inp["group_norm_b"][0])]).reshape(8, 128).T),
        "w_out": f(inp["w_out"][0]),
        "lnp": f(np.stack([np.asarray(inp["ln1_g"][0]), np.asarray(inp["ln1_b"][0]), np.asarray(inp["ln2_g"][0]), np.asarray(inp["ln2_b"][0])])),
        "w_q": f(inp["peer_w_q"][0]), "subk": f(np.asarray(inp["peer_sub_keys"][0]).reshape(16, 128, 128)),
        "u": f(inp["peer_u"][0]), "v": f(inp["peer_v"][0]),
    }
    consts = [host_consts(0), host_consts(1)]
    maps = []
    for core in range(2 * Bn):
        b, par = core // 2, core % 2
        xb = x[b].reshape(NSB, 512, D)
        own = list(range(par, NSB, 2))
        halo = np.zeros((len(own), 128, D), np.float32)
        for k, s in enumerate(own):
            if s > 0:
                halo[k] = xb[s - 1, 384:512]
        m = dict(common)
        m.update(consts[par])
        m["xall"] = f(x[b])
        m["xown"] = f(xb[own].reshape(-1, D))
        m["xhalo"] = f(halo.reshape(-1, D))
        m["cT"] = f(np.asarray(inp["c"])[b].reshape(8, 128).T)
        maps.append(m)
    return maps


def assemble(results, Bn, NSB):
    outp = np.zeros((Bn, NSB, 512, D), np.float32)
    for core in range(2 * Bn):
        b, par = core // 2, core % 2
        own = list(range(par, NSB, 2))
        outp[b, own] = np.asarray(results[core]["out"]).reshape(len(own), 512, D)
    return outp.reshape(Bn, NSB * 512, D)


def kernel(**inputs):
    x = np.asarray(inputs["x"])
    Bn, S, _ = x.shape
    NSB = S // 512
    nc = build(NSB)
    maps = make_in_maps(inputs, NSB)
    res = run_bass_kernel_spmd(nc, maps, core_ids=list(range(2 * Bn)))
    return assemble(res.results, Bn, NSB)
```

```python
import numpy as np
import ml_dtypes
import concourse.bass as bass
import concourse.mybir as mybir
from concourse.bass_utils import run_bass_kernel_spmd

F32 = mybir.dt.float32
BF16 = mybir.dt.bfloat16
U32 = mybir.dt.uint32
AF = mybir.ActivationFunctionType
ALU = mybir.AluOpType
AX = mybir.AxisListType
ENGS = ["pe", "act", "dve", "pool", "sp"]
D = 1024
ALPHA = 2.0 ** 0.25
EPS = 1e-5
NEG = -30000.0


class Buf:
    __slots__ = ("name", "lw", "rd", "dsem", "dcnt")

    def __init__(self, name):
        self.name = name
        self.lw = None
        self.rd = []
        self.dsem = {}
        self.dcnt = {}


class Op:
    __slots__ = ("eng", "fn", "idx", "inc", "is_dma", "dsem", "dval", "waits", "cnt")


class Prog:
    def __init__(self):
        self.ops = {e: [] for e in ENGS}
        self.ndsem = 0
        self.clock = {e: {} for e in ENGS}
        self.dmas = []

    def _mk(self, eng, fn, reads, writes, is_dma, extra=()):
        o = Op()
        o.eng = eng
        o.fn = fn
        o.idx = len(self.ops[eng])
        o.inc = False
        o.is_dma = is_dma
        o.dsem = None
        o.dval = 0
        deps = list(extra)
        for b in reads:
            if b.lw is not None:
                deps.append(b.lw)
        for b in writes:
            if b.lw is not None:
                deps.append(b.lw)
            deps.extend(b.rd)
        ck = self.clock[eng]
        waits = []
        for d in deps:
            if d.is_dma:
                key = ("d", d.dsem)
                if ck.get(key, 0) >= d.dval:
                    continue
                ck[key] = d.dval
                waits.append(("d", d.dsem, d.dval))
            else:
                if d.eng == eng and eng == "pe":
                    continue
                if ck.get(d.eng, -1) >= d.idx:
                    continue
                ck[d.eng] = d.idx
                d.inc = True
                waits.append(("e", d.eng, d))
        o.waits = waits
        for b in reads:
            b.rd.append(o)
        for b in writes:
            b.lw = o
            b.rd = []
        self.ops[eng].append(o)
        return o

    def op(self, eng, fn, reads=(), writes=()):
        return self._mk(eng, fn, reads, writes, False)

    def dma(self, eng, out_ap, in_ap, sb, reads=(), writes=()):
        def fn(e):
            return e.dma_start(out=out_ap, in_=in_ap)
        o = self._mk(eng, fn, reads, writes, True)
        kind = "sw" if eng == "pool" else "hw"
        if kind not in sb.dsem:
            sb.dsem[kind] = self.ndsem
            sb.dcnt[kind] = 0
            self.ndsem += 1
        sb.dcnt[kind] += 16
        o.dsem = sb.dsem[kind]
        o.dval = sb.dcnt[kind]
        self.dmas.append(o)
        return o

    def barrier(self):
        lasts = [self.ops[e][-1] for e in ENGS if self.ops[e] and not self.ops[e][-1].is_dma]
        lasts = []
        for e in ENGS:
            for o in reversed(self.ops[e]):
                if not o.is_dma and o.fn is not None:
                    lasts.append(o)
                    break
        best = {}
        for o in self.dmas:
            if o.dsem not in best or best[o.dsem].dval < o.dval:
                best[o.dsem] = o
        dm = list(best.values())
        self.dmas = []
        for e in ENGS:
            self._mk(e, None, (), (), False, extra=lasts + dm)

    def emit(self, nc):
        self.barrier()
        for e in ENGS:
            c = 0
            for o in self.ops[e]:
                if o.inc and not o.is_dma:
                    c += 1
                o.cnt = c
        import contextlib
        with contextlib.ExitStack() as st:
            esem = {e: st.enter_context(nc.semaphore("s_" + e)) for e in ENGS}
            dsem = [st.enter_context(nc.semaphore("d%d" % i)) for i in range(self.ndsem)]
            block = st.enter_context(nc.Block())
            ops = self.ops

            def run(eng_name, e):
                for o in ops[eng_name]:
                    for w in o.waits:
                        if w[0] == "d":
                            e.wait_ge(dsem[w[1]], w[2])
                        else:
                            e.wait_ge(esem[w[1]], w[2].cnt)
                    if o.fn is None:
                        continue
                    ins = o.fn(e)
                    if o.is_dma:
                        ins.then_inc(dsem[o.dsem], 16)
                    elif o.inc:
                        ins.then_inc(esem[eng_name], 1)

            @block.tensor
            def _(e):
                run("pe", e)

            @block.scalar
            def _(e):
                run("act", e)

            @block.vector
            def _(e):
                run("dve", e)

            @block.gpsimd
            def _(e):
                run("pool", e)

            @block.sync
            def _(e):
                run("sp", e)


def V(a, off, dims):
    return bass.AP(tensor=a.tensor, offset=a.offset + off, ap=[list(a.ap[0])] + [list(d) for d in dims])


def build(NSB, NE=16384, dbg=False):
    S = 512 * NSB
    NOWN = NSB // 2
    T = 512 * NOWN
    NKB = S // 128
    NCH = NE // 128
    nc = bass.Bass("TRN2", target_bir_lowering=False)
    P = Prog()

    def din(name, shape, dt=F32):
        return nc.dram_tensor(name, shape, dt, kind="ExternalInput").ap()

    xall = din("xall", [S, D])
    xown = din("xown", [T, D])
    xhalo = din("xhalo", [NOWN * 128, D])
    cT = din("cT", [128, 8])
    w_ada = din("w_ada", [D, 6 * D])
    b_adaT = din("b_adaT", [128, 48])
    w_in = din("w_in", [D, 2304])
    sinks = din("sinks", [1, 8])
    gnT = din("gnT", [128, 8])
    w_out = din("w_out", [D, D])
    lnp = din("lnp", [4, D])
    w_q = din("w_q", [D, 2048])
    subk = din("subk", [16, 128, 128])
    u_in = din("u", [NE, D])
    v_in = din("v", [NE, D])
    c_ident = din("c_ident", [128, 128])
    c_ntri = din("c_ntri", [128, 128])
    c_iota = din("c_iota", [128, 128])
    c_bm = din("c_bm", [128, 8])
    c_sbmask = din("c_sbmask", [128, 2 * 4 * 512])
    c_swab = din("c_swab", [128, 2 * 8 * 256])
    out = nc.dram_tensor("out", [T, D], F32, kind="ExternalOutput").ap()

    def scratch(name, shape, dt):
        return nc.dram_tensor(name, shape, dt, kind="Internal").ap()

    HTall = scratch("HTall", [8, 128, S], BF16)
    HTown = scratch("HTown", [8, 128, T], BF16)
    HThalo = scratch("HThalo", [8, 128, NOWN * 128], BF16)
    OTs = scratch("OTs", [8, 128, T], BF16)
    X1 = scratch("X1", [T, D], F32)
    H2T = scratch("H2T", [8, 128, T], BF16)
    IDX1T = scratch("IDX1T", [128, T], F32)
    IDX2T = scratch("IDX2T", [128, T], F32)
    GT = scratch("GT", [128, 16, T], F32)
    UT = scratch("UT", [8, 128, NE], BF16)
    VB = scratch("VB", [NE, D], BF16)
    DB = {n: Buf(n) for n in ["HTall", "HTown", "HThalo", "OTs", "X1", "H2T", "IDX1T", "IDX2T", "GT", "UT", "VB", "out"]}

    AW = 52800
    arena = nc.alloc_sbuf_tensor("arena", [128, AW], F32)
    st = {"off": 0, "n": 0}

    def alloc(free_shape, dt=F32, name=None):
        n = int(np.prod(free_shape))
        words = n if dt in (F32, U32) else (n + 1) // 2
        words = (words + 7) // 8 * 8
        off = st["off"]
        st["off"] += words
        assert st["off"] <= AW, ("SBUF arena overflow", st["off"])
        a = arena[:, off:off + words]
        if dt != F32:
            a = a.bitcast(dt)
        a = a[:, 0:n]
        if len(free_shape) == 2:
            a = a.rearrange("p (a b) -> p a b", a=free_shape[0])
        elif len(free_shape) == 3:
            a = a.rearrange("p (a b c) -> p a b c", a=free_shape[0], b=free_shape[1])
        st["n"] += 1
        return a, Buf(name or "b%d" % st["n"])

    class Rot:
        def __init__(self, n, shape, dt=F32):
            self.items = [alloc(shape, dt) for _ in range(n)]
            self.i = 0

        def next(self):
            it = self.items[self.i % len(self.items)]
            self.i += 1
            return it

    ps = [nc.alloc_psum_tensor("ps%d" % i, [128, 512], F32) for i in range(8)]
    PB = [Buf("ps%d" % i) for i in range(8)]
    pst = {"i": 0}

    def bank():
        i = pst["i"] % 8
        pst["i"] += 1
        return ps[i][:, :], PB[i]

    def mm_group(outs, reads, writes):
        def fn(e, outs=outs):
            r = None
            for (o, l, rr, s0, s1) in outs:
                r = e.matmul(o, lhsT=l, rhs=rr, start=s0, stop=s1)
            return r
        P.op("pe", fn, reads=reads, writes=writes)

    def act(out_ap, in_ap, func, reads, writes, bias=0.0, scale=1.0, accum=None):
        def fn(e):
            if accum is not None:
                return e.activation(out=out_ap, in_=in_ap, func=func, bias=bias, scale=scale, accum_out=accum)
            return e.activation(out=out_ap, in_=in_ap, func=func, bias=bias, scale=scale)
        P.op("act", fn, reads=reads, writes=writes)

    def tt(eng, out_ap, a, b, op, reads, writes):
        P.op(eng, lambda e: e.tensor_tensor(out=out_ap, in0=a, in1=b, op=op), reads=reads, writes=writes)

    def ts(eng, out_ap, a, s1, s2, op0, op1, reads, writes, accum=None):
        def fn(e):
            if accum is not None:
                return e.tensor_scalar(out=out_ap, in0=a, scalar1=s1, scalar2=s2, op0=op0, op1=op1, accum_out=accum)
            if op1 is None:
                return e.tensor_scalar(out=out_ap, in0=a, scalar1=s1, scalar2=None, op0=op0)
            return e.tensor_scalar(out=out_ap, in0=a, scalar1=s1, scalar2=s2, op0=op0, op1=op1)
        P.op(eng, fn, reads=reads, writes=writes)

    def cp(eng, out_ap, in_ap, reads, writes):
        if eng == "act":
            P.op("act", lambda e: e.copy(out=out_ap, in_=in_ap), reads=reads, writes=writes)
        else:
            P.op(eng, lambda e: e.tensor_copy(out=out_ap, in_=in_ap), reads=reads, writes=writes)

    def load(eng, dst, src, buf, extra_reads=()):
        P.dma(eng, dst, src, buf, reads=list(extra_reads), writes=[buf])

    def store(dst, src, buf, dbuf):
        P.dma("sp", dst, src, buf, reads=[buf], writes=[dbuf])

    ident, b_ident = alloc([128])
    identb, b_identb = alloc([128], BF16)
    onesf, b_onesf = alloc([128])
    onesb, b_onesb = alloc([128], BF16)
    ntri, b_ntri = alloc([128], BF16)
    nones, b_nones = alloc([128], BF16)
    iota, b_iota = alloc([128])
    bm, b_bm = alloc([8])
    ada, b_ada = alloc([48])
    s1p1, b_s1p1 = alloc([8])
    s2p1c, b_s2p1c = alloc([8])
    gate1_bc, b_g1 = alloc([D])
    s2p1_bc, b_s2 = alloc([D])
    shift2_bc, b_sh2 = alloc([D])
    gate2_bc, b_g2 = alloc([D])
    lnp_bc, b_lnp = alloc([4, D])
    sink_bc, b_sink = alloc([8])
    dummy, b_dummy = alloc([8])

    load("sp", ident, c_ident, b_ident)
    load("pool", identb, c_ident, b_identb)
    load("pool", ntri, c_ntri, b_ntri)
    load("sp", iota, c_iota, b_iota)
    load("sp", bm, c_bm, b_bm)
    P.op("pool", lambda e: e.memset(onesf, 1.0), writes=[b_onesf])
    P.op("pool", lambda e: e.memset(onesb, 1.0), writes=[b_onesb])
    P.op("pool", lambda e: e.memset(nones, -1.0), writes=[b_nones])
    load("sp", lnp_bc, bass.AP(tensor=lnp.tensor, offset=0, ap=[[0, 128], [D, 4], [1, D]]), b_lnp)
    load("sp", sink_bc, bass.AP(tensor=sinks.tensor, offset=0, ap=[[0, 128], [1, 8]]), b_sink)
    PERM_END = st["off"]

    sc_, b_sc = alloc([8])
    csb, b_csb = alloc([8])
    bada, b_bada = alloc([48])
    load("sp", csb, cT, b_csb)
    load("sp", bada, b_adaT, b_bada)
    act(sc_, csb, AF.Silu, [b_csb], [b_sc])
    wa = Rot(2, [8, 1024])
    pa, bpa = bank()
    for gi in range(6):
        wt, bwt = wa.next()
        load("sp", wt, w_ada[:, gi * 1024:(gi + 1) * 1024].rearrange("(c p) n -> p c n", p=128), bwt)
        for jj in range(8):
            j = gi * 8 + jj
            mm_group([(pa[:, j:j + 1], wt[:, dc, jj * 128:(jj + 1) * 128], sc_[:, dc:dc + 1], dc == 0, dc == 7) for dc in range(8)],
                     reads=[bwt, b_sc], writes=[bpa])
    tt("dve", ada, pa[:, 0:48], bada, ALU.add, [bpa, b_bada], [b_ada])
    ts("dve", s1p1, ada[:, 8:16], 1.0, None, ALU.add, None, [b_ada], [b_s1p1])
    ts("dve", s2p1c, ada[:, 32:40], 1.0, None, ALU.add, None, [b_ada], [b_s2p1c])
    shift1 = ada[:, 0:8]
    diag = Rot(2, [128])

    def bcast_row(dst, bdst, col, bcol):
        for half in range(2):
            pb_, bpb_ = bank()
            for q in range(4):
                dc = half * 4 + q
                dg, bdg = diag.next()
                ts("dve", dg, ident, col[:, dc:dc + 1], None, ALU.mult, None, [b_ident, bcol], [bdg])
                mm_group([(pb_[:, q * 128:(q + 1) * 128], onesf, dg, True, True)], reads=[b_onesf, bdg], writes=[bpb_])
            cp("act", dst[:, half * 512:(half + 1) * 512], pb_, [bpb_], [bdst])

    bcast_row(gate1_bc, b_g1, ada[:, 16:24], b_ada)
    bcast_row(s2p1_bc, b_s2, s2p1c, b_s2p1c)
    bcast_row(shift2_bc, b_sh2, ada[:, 24:32], b_ada)
    bcast_row(gate2_bc, b_g2, ada[:, 40:48], b_ada)

    xt = Rot(3, [D])
    hTt = Rot(2, [8, 512], BF16)
    hT_aux = {}

    def make_hT(src, ntok, dst, bdst):
        for t0 in range(0, ntok, 512):
            n = min(512, ntok - t0)
            ht, bht = hTt.next()
            bht2 = hT_aux.setdefault(id(bht), Buf("ht_aux"))
            for s in range(n // 128):
                x_, bx = xt.next()
                load("sp", x_, src[t0 + s * 128:t0 + (s + 1) * 128, :], bx)
                for half in range(2):
                    pb_, bpb_ = bank()
                    P.op("pe", lambda e, pb_=pb_, x_=x_, half=half: [e.transpose(out=pb_[:, q * 128:(q + 1) * 128], in_=x_[:, (half * 4 + q) * 128:(half * 4 + q + 1) * 128], identity=ident) for q in range(4)][-1],
                         reads=[bx, b_ident], writes=[bpb_])
                    for q in range(4):
                        dc = half * 4 + q
                        if True:
                            act(ht[:, dc, s * 128:(s + 1) * 128], pb_[:, q * 128:(q + 1) * 128], AF.Identity, [bpb_, b_s1p1, b_ada], [bht],
                                bias=shift1[:, dc:dc + 1], scale=s1p1[:, dc:dc + 1])
                        else:
                            ts("dve", ht[:, dc, s * 128:(s + 1) * 128], pb_[:, q * 128:(q + 1) * 128], s1p1[:, dc:dc + 1], shift1[:, dc:dc + 1], ALU.mult, ALU.add,
                               [bpb_, b_s1p1, b_ada], [bht2])
            P.dma("sp", dst[:, :, t0:t0 + n].rearrange("c p t -> p c t"), ht[:, :, 0:n], bht, reads=[bht, bht2], writes=[bdst])

    make_hT(xall, S, HTall, DB["HTall"])
    make_hT(xown, T, HTown, DB["HTown"])
    make_hT(xhalo, NOWN * 128, HThalo, DB["HThalo"])
    P.barrier()
    st["off"] = PERM_END

    def wload(cols0, ncols, name):
        w, bw = alloc([8, ncols], BF16, name)
        load("pool", w, w_in[:, cols0:cols0 + ncols].rearrange("(c p) n -> p c n", p=128), bw)
        return w, bw

    def proj_fm(dst, bdst, w, bw, wc0, hT, bhT, t0, n, scale=1.0, eng="act"):
        pb_, bpb_ = bank()
        mm_group([(pb_[:, 0:n], w[:, dc, wc0:wc0 + 128], hT[:, dc, t0:t0 + n], dc == 0, dc == 7) for dc in range(8)],
                 reads=[bw, bhT], writes=[bpb_])
        if eng == "act":
            act(dst, pb_[:, 0:n], AF.Copy, [bpb_], [bdst], scale=scale)
        else:
            cp(eng, dst, pb_[:, 0:n], [bpb_], [bdst])

    wqa, bwqa = wload(0, 512, "wqa")
    wkd, bwkd = alloc([8, 256], BF16, "wkd")
    wvd, bwvd = alloc([8, 256], BF16, "wvd")
    for g in range(2):
        for dup in range(2):
            load("pool", wkd[:, :, (g * 2 + dup) * 64:(g * 2 + dup + 1) * 64], w_in[:, 512 + g * 64:512 + (g + 1) * 64].rearrange("(c p) n -> p c n", p=128), bwkd)
            load("pool", wvd[:, :, (g * 2 + dup) * 64:(g * 2 + dup + 1) * 64], w_in[:, 640 + g * 64:640 + (g + 1) * 64].rearrange("(c p) n -> p c n", p=128), bwvd)
    swab, b_swab = alloc([2, 8, 256], F32, "swab")
    load("sp", swab, c_swab.rearrange("p (a h k) -> p a h k", a=2, h=8), b_swab)
    hTc, bhTc_h = alloc([8, 640], BF16, "hTc")
    bhTc_o = Buf("hTc_o")
    qaT, bqaT = alloc([4, 512], BF16)
    kaT, bkaT = alloc([2, 640], BF16)
    vad, bvad = alloc([5, 256], BF16)
    oTa, boTa = alloc([4, 512], BF16)
    r_sc1 = Rot(8, [256])
    r_p = Rot(8, [256])
    r_pn = Rot(8, [256], BF16)
    r_pT = Rot(8, [2, 128], BF16)
    r_small = Rot(8, [8])
    for i in range(NOWN):
        load("sp", hTc[:, :, 0:128], HThalo[:, :, i * 128:(i + 1) * 128].rearrange("c p t -> p c t"), bhTc_h, extra_reads=[DB["HThalo"]])
        load("sp", hTc[:, :, 128:640], HTown[:, :, i * 512:(i + 1) * 512].rearrange("c p t -> p c t"), bhTc_o, extra_reads=[DB["HTown"]])
        bh = Buf("hTc_both")
        for j in range(4):
            pb_, bpb_ = bank()
            mm_group([(pb_, wqa[:, dc, j * 128:(j + 1) * 128], hTc[:, dc, 128:640], dc == 0, dc == 7) for dc in range(8)],
                     reads=[bwqa, bhTc_o], writes=[bpb_])
            act(qaT[:, j, :], pb_, AF.Copy, [bpb_], [bqaT], scale=0.125)
        for g in range(2):
            for ch in range(2):
                pb_, bpb_ = bank()
                mm_group([(pb_[:, 0:320], wkd[:, dc, g * 128:(g + 1) * 128], hTc[:, dc, ch * 320:(ch + 1) * 320], dc == 0, dc == 7) for dc in range(8)],
                         reads=[bwkd, bhTc_o, bhTc_h], writes=[bpb_])
                cp("dve", kaT[:, g, ch * 320:(ch + 1) * 320], pb_[:, 0:320], [bpb_], [bkaT])
        for blk in range(5):
            pb_, bpb_ = bank()
            mm_group([(pb_[:, 0:256], hTc[:, dc, blk * 128:(blk + 1) * 128], wvd[:, dc, :], dc == 0, dc == 7) for dc in range(8)],
                     reads=[bwvd, bhTc_o, bhTc_h], writes=[bpb_])
            cp("dve", vad[:, blk, :], pb_[:, 0:256], [bpb_], [bvad])
        for r in range(4):
            var = 0 if (i == 0 and r == 0) else 1
            H8 = range(8)
            hb_ = [(h % 2) * 64 for h in H8]
            pr_ = [h // 2 for h in H8]
            g_ = [h // 4 for h in H8]
            pbs = []
            for h in H8:
                pb_, bpb_ = bank()
                mm_group([(pb_[:, 0:256], qaT[hb_[h]:hb_[h] + 64, pr_[h], r * 128:(r + 1) * 128], kaT[hb_[h]:hb_[h] + 64, g_[h], r * 128:r * 128 + 256], True, True)],
                         reads=[bqaT, bkaT], writes=[bpb_])
                pbs.append((pb_, bpb_))
            scs = [r_sc1.next() for h in H8]
            sms = [r_small.next() for h in H8]
            pps = [r_p.next() for h in H8]
            pns = [r_pn.next() for h in H8]
            pTs = [r_pT.next() for h in H8]
            for h in H8:
                tt("dve", scs[h][0], pbs[h][0][:, 0:256], swab[:, var, h, :], ALU.add, [pbs[h][1], b_swab], [scs[h][1]])
            for h in H8:
                P.op("dve", lambda e, sm=sms[h][0], sc1=scs[h][0]: e.reduce_max(out=sm[:, 0:1], in_=sc1, axis=AX.X), reads=[scs[h][1]], writes=[sms[h][1]])
            for h in H8:
                sm, bsm = sms[h]
                ts("dve", sm[:, 1:2], sm[:, 0:1], sink_bc[:, h:h + 1], -1.0, ALU.max, ALU.mult, [bsm, b_sink], [bsm])
            for h in H8:
                sm, bsm = sms[h]
                act(pps[h][0], scs[h][0], AF.Exp, [scs[h][1], bsm], [pps[h][1], bsm], bias=sm[:, 1:2], accum=sm[:, 2:3])
            for h in H8:
                sm, bsm = sms[h]
                act(sm[:, 3:4], sink_bc[:, h:h + 1], AF.Exp, [b_sink, bsm], [bsm], bias=sm[:, 1:2])
            for h in H8:
                sm, bsm = sms[h]
                tt("dve", sm[:, 4:5], sm[:, 2:3], sm[:, 3:4], ALU.add, [bsm], [bsm])
            for h in H8:
                sm, bsm = sms[h]
                P.op("dve", lambda e, sm=sm: e.reciprocal(out=sm[:, 5:6], in_=sm[:, 4:5]), reads=[bsm], writes=[bsm])
            for h in H8:
                sm, bsm = sms[h]
                ts("dve", pns[h][0], pps[h][0], sm[:, 5:6], None, ALU.mult, None, [pps[h][1], bsm], [pns[h][1]])
            pb2s = []
            for h in H8:
                pb2, bpb2 = bank()
                pb2b = pb2.bitcast(BF16)
                P.op("pe", lambda e, pb2b=pb2b, pn=pns[h][0]: [e.transpose(out=pb2b[:, kb * 128:(kb + 1) * 128], in_=pn[:, kb * 128:(kb + 1) * 128], identity=identb) for kb in range(2)][-1],
                     reads=[pns[h][1], b_identb], writes=[bpb2])
                pb2s.append((pb2b, bpb2))
            for h in H8:
                cp("act", pTs[h][0], pb2s[h][0][:, 0:256].rearrange("p (a b) -> p a b", a=2), [pb2s[h][1]], [pTs[h][1]])
            pb3s = []
            for h in H8:
                pb3, bpb3 = bank()
                mm_group([(pb3[:, 0:128], vad[:, r + kb, g_[h] * 128:(g_[h] + 1) * 128], pTs[h][0][:, kb, :], kb == 0, kb == 1) for kb in range(2)],
                         reads=[bvad, pTs[h][1]], writes=[bpb3])
                pb3s.append((pb3, bpb3))
            for h in H8:
                cp("act", oTa[hb_[h]:hb_[h] + 64, pr_[h], r * 128:(r + 1) * 128], pb3s[h][0][hb_[h]:hb_[h] + 64, 0:128], [pb3s[h][1]], [boTa])
        store(OTs[0:4, :, i * 512:(i + 1) * 512].rearrange("c p t -> p c t"), oTa, boTa, DB["OTs"])
    P.barrier()
    st["off"] = PERM_END

    sbmask, b_sbmask = alloc([2, 4, 512], F32, "sbmask")
    load("sp", sbmask, c_sbmask.rearrange("p (a k q) -> p a k q", a=2, k=4), b_sbmask)
    kT, bkT = alloc([2, S], BF16, "kT")
    vv, bvv = alloc([NKB, 256], BF16, "vv")
    A2_END = st["off"]
    for hp2 in range(2):
        st["off"] = A2_END
        wqb, bwqb = wload(768 + hp2 * 256, 256, "wqb")
        wkb, bwkb = wload(1280 + hp2 * 256, 256, "wkb")
        wvb, bwvb = wload(1792 + hp2 * 256, 256, "wvb")
        hTl = Rot(2, [8, 512], BF16)
        for tt_ in range(NSB):
            ht, bht = hTl.next()
            load("sp", ht, HTall[:, :, tt_ * 512:(tt_ + 1) * 512].rearrange("c p t -> p c t"), bht, extra_reads=[DB["HTall"]])
            for pr in range(2):
                proj_fm(kT[:, pr, tt_ * 512:(tt_ + 1) * 512], bkT, wkb, bwkb, pr * 128, ht, bht, 0, 512)
            for s in range(4):
                pb_, bpb_ = bank()
                mm_group([(pb_[:, 0:256], ht[:, dc, s * 128:(s + 1) * 128], wvb[:, dc, :], dc == 0, dc == 7) for dc in range(8)],
                         reads=[bwvb, bht], writes=[bpb_])
                cp("dve", vv[:, tt_ * 4 + s, :], pb_[:, 0:256], [bpb_], [bvv])
        qbT, bqbT = alloc([2, 512], BF16)
        oTb, boTb = alloc([2, 512], BF16)
        r_e = Rot(8, [512])
        r_sp = Rot(12, [512], BF16)
        r_a = Rot(8, [512], BF16)
        Rl = [alloc([512]) for _ in range(4)]
        Rbl = [[alloc([512], BF16) for _ in range(2)] for _ in range(4)]
        for i in range(NOWN):
            ht, bht = hTl.next()
            load("sp", ht, HTown[:, :, i * 512:(i + 1) * 512].rearrange("c p t -> p c t"), bht, extra_reads=[DB["HTown"]])
            for pr in range(2):
                proj_fm(qbT[:, pr, :], bqbT, wqb, bwqb, pr * 128, ht, bht, 0, 512, scale=0.125)
            nkb = 8 * i + 8
            for hl in range(4):
                R, bR = Rl[hl]
                P.op("pool", lambda e, R=R: e.memset(R, 0.0), writes=[bR])
                for pp_ in range(2):
                    Rb, bRb = Rbl[hl][pp_]
                    P.op("pool", lambda e, Rb=Rb: e.memset(Rb, 0.0), writes=[bRb])
            pos = [bank() for _ in range(4)]
            pobufs = [b for (_, b) in pos]

            def fbank():
                while True:
                    p_, b_ = bank()
                    if b_ not in pobufs:
                        return p_, b_

            def stage1(kb):
                masked = kb >= 8 * i
                if masked:
                    mk = sbmask[:, (kb - 8 * i) // 4, (kb - 8 * i) % 4, :]
                sps = []
                for hl in range(4):
                    pr = hl // 2
                    hb = (hl % 2) * 64
                    ksl = kT[hb:hb + 64, pr, kb * 128:(kb + 1) * 128]
                    qsl = qbT[hb:hb + 64, pr, :]
                    pz, bpz = fbank()
                    mm_group([(pz, ksl, qsl, True, True)], reads=[bkT, bqbT], writes=[bpz])
                    e_, be = r_e.next()
                    act(e_, pz, AF.Exp, [bpz], [be])
                    sp_, bsp = r_sp.next()
                    act(sp_, e_, AF.Ln, [be], [bsp], bias=1.0)
                    if masked:
                        tt("dve", sp_, sp_, mk, ALU.mult, [bsp, b_sbmask], [bsp])
                    sps.append((sp_, bsp))
                return sps

            def stage2(kb, sps):
                masked = kb >= 8 * i
                if masked:
                    mk = sbmask[:, (kb - 8 * i) // 4, (kb - 8 * i) % 4, :]
                avs = []
                for hl in range(4):
                    pr = hl // 2
                    hb = (hl % 2) * 64
                    ksl = kT[hb:hb + 64, pr, kb * 128:(kb + 1) * 128]
                    qsl = qbT[hb:hb + 64, pr, :]
                    sp_, bsp = sps[hl]
                    R, bR = Rl[hl]
                    Rb, bRb = Rbl[hl][kb % 2]
                    Rbn, bRbn = Rbl[hl][(kb + 1) % 2]
                    po, bpo = pos[hl]
                    if kb > 0:
                        tt("dve", Rbn, R, sp_, ALU.add, [bR, bsp], [bRbn])
                        tt("dve", R, R, sp_, ALU.add, [bR, bsp], [bR])
                    pp, bpp = fbank()
                    mm_group([(pp, ksl, qsl, True, False), (pp, ntri, sp_, False, False), (pp, nones, Rb, False, True)],
                             reads=[bkT, bqbT, b_ntri, bsp, b_nones, bRb], writes=[bpp])
                    a_, ba = r_a.next()
                    act(a_, pp, AF.Exp, [bpp], [ba])
                    if masked:
                        tt("pool", a_, a_, mk, ALU.mult, [ba, b_sbmask], [ba])
                    avs.append((po, bpo, pr, a_, ba))
                for (po, bpo, pr, a_, ba) in avs:
                    mm_group([(po, vv[:, kb, pr * 128:(pr + 1) * 128], a_, kb == nkb - 1, kb == 0)], reads=[bvv, ba], writes=[bpo])

            cur = stage1(nkb - 1)
            for kb in range(nkb - 1, -1, -1):
                nxt = stage1(kb - 1) if kb > 0 else None
                stage2(kb, cur)
                cur = nxt
            for hl in range(4):
                pr = hl // 2
                hb = (hl % 2) * 64
                po, bpo = pos[hl]
                cp("act", oTb[hb:hb + 64, pr, :], po[hb:hb + 64, :], [bpo], [boTb])
            store(OTs[4 + hp2 * 2:6 + hp2 * 2, :, i * 512:(i + 1) * 512].rearrange("c p t -> p c t"), oTb, boTb, DB["OTs"])
        P.barrier()
    st["off"] = PERM_END

    wos, bwos = alloc([8, D], BF16, "wos")
    gn, bgn = alloc([8], F32, "gn")
    load("sp", gn, gnT, bgn)
    wstage = Rot(2, [D])
    for ch in range(8):
        wsg, bwsg = wstage.next()
        load("sp", wsg, w_out[ch * 128:(ch + 1) * 128, :], bwsg)
        ts("dve", wos[:, ch, :], wsg, gn[:, ch:ch + 1], None, ALU.mult, None, [bwsg, bgn], [bwos])
    oTl = Rot(2, [8, 512], BF16)
    sql = Rot(2, [8, 512], BF16)
    xl = Rot(2, [D])
    t1l = Rot(2, [D])
    t2l = Rot(2, [D])
    sml = Rot(4, [16])
    stl = Rot(2, [12])

    def layer_norm_tile(y, by, dst, bdst, gi, eng2="pool"):
        stt, bst = stl.next()
        for hh in range(2):
            P.op("dve", lambda e, stt=stt, y=y, hh=hh: e.bn_stats(out=stt[:, hh * 6:(hh + 1) * 6], in_=y[:, hh * 512:(hh + 1) * 512]), reads=[by], writes=[bst])
        sm, bsm = sml.next()
        P.op("dve", lambda e, sm=sm, stt=stt: e.bn_aggr(out=sm[:, 0:2], in_=stt), reads=[bst], writes=[bsm])
        ts("dve", sm[:, 2:3], sm[:, 1:2], EPS, None, ALU.add, None, [bsm], [bsm])
        P.op("act", lambda e, sm=sm: e.sqrt(out=sm[:, 3:4], in_=sm[:, 2:3]), reads=[bsm], writes=[bsm])
        P.op("dve", lambda e, sm=sm: e.reciprocal(out=sm[:, 4:5], in_=sm[:, 3:4]), reads=[bsm], writes=[bsm])
        ts("dve", y, y, sm[:, 0:1], sm[:, 4:5], ALU.subtract, ALU.mult, [by, bsm], [by])
        tt(eng2, y, y, lnp_bc[:, gi, :], ALU.mult, [by, b_lnp], [by])
        tt(eng2, dst, y, lnp_bc[:, gi + 1, :], ALU.add, [by, b_lnp], [bdst])

    for i in range(NOWN):
        oT, boT = oTl.next()
        load("sp", oT, OTs[:, :, i * 512:(i + 1) * 512].rearrange("c p t -> p c t"), boT, extra_reads=[DB["OTs"]])
        sq, bsq = sql.next()
        tt("pool", sq, oT, oT, ALU.mult, [boT], [bsq])
        for s in range(4):
            tok = slice(s * 128, (s + 1) * 128)
            pq, bpq = bank()
            mm_group([(pq[:, grp:grp + 1], sq[:, grp * 4 + ch, tok], onesb[:, 0:1], ch == 0, ch == 3) for grp in range(2) for ch in range(4)],
                     reads=[bsq, b_onesb], writes=[bpq])
            sm, bsm = sml.next()
            ts("dve", sm[:, 0:2], pq[:, 0:2], 1.0 / 512, EPS, ALU.mult, ALU.add, [bpq], [bsm])
            P.op("act", lambda e, sm=sm: e.sqrt(out=sm[:, 2:4], in_=sm[:, 0:2]), reads=[bsm], writes=[bsm])
            P.op("dve", lambda e, sm=sm: e.reciprocal(out=sm[:, 4:6], in_=sm[:, 2:4]), reads=[bsm], writes=[bsm])
            x_, bx = xl.next()
            load("sp", x_, xown[i * 512 + s * 128:i * 512 + (s + 1) * 128, :], bx)
            t1, bt1 = t1l.next()
            t2, bt2 = t2l.next()
            for half in range(2):
                hs = slice(half * 512, (half + 1) * 512)
                pa_, bpa_ = bank()
                mm_group([(pa_, oT[:, ch, tok], wos[:, ch, hs], ch == 0, ch == 3) for ch in range(4)], reads=[boT, bwos], writes=[bpa_])
                pbb, bpbb = bank()
                mm_group([(pbb, oT[:, 4 + ch, tok], wos[:, 4 + ch, hs], ch == 0, ch == 3) for ch in range(4)], reads=[boT, bwos], writes=[bpbb])
                act(t1[:, hs], pa_, AF.Copy, [bpa_, bsm], [bt1], scale=sm[:, 4:5])
                P.op("dve", lambda e, t2=t2, pbb=pbb, sm=sm, t1=t1, hs=hs: e.scalar_tensor_tensor(out=t2[:, hs], in0=pbb, scalar=sm[:, 5:6], in1=t1[:, hs], op0=ALU.mult, op1=ALU.add),
                     reads=[bpbb, bsm, bt1], writes=[bt2])
            tt("pool", t2, t2, gate1_bc, ALU.mult, [bt2, b_g1], [bt2])
            P.op("dve", lambda e, t1=t1, x_=x_, t2=t2: e.scalar_tensor_tensor(out=t1, in0=x_, scalar=ALPHA, in1=t2, op0=ALU.mult, op1=ALU.add),
                 reads=[bx, bt2, bt1], writes=[bt1])
            layer_norm_tile(t1, bt1, t2, bt2, 0)
            store(X1[i * 512 + s * 128:i * 512 + (s + 1) * 128, :], t2, bt2, DB["X1"])
    P.barrier()
    st["off"] = PERM_END

    ul = Rot(2, [D])
    utl = Rot(2, [8, 512], BF16)
    vl = Rot(2, [4, D], BF16)
    for sc4 in range(NCH // 4):
        ut, but = utl.next()
        for cl in range(4):
            c = sc4 * 4 + cl
            u_, bu = ul.next()
            load("sp", u_, u_in[c * 128:(c + 1) * 128, :], bu)
            for half in range(2):
                pb_, bpb_ = bank()
                P.op("pe", lambda e, pb_=pb_, u_=u_, half=half: [e.transpose(out=pb_[:, q * 128:(q + 1) * 128], in_=u_[:, (half * 4 + q) * 128:(half * 4 + q + 1) * 128], identity=ident) for q in range(4)][-1],
                     reads=[bu, b_ident], writes=[bpb_])
                cp("act" if half == 0 else "dve", ut[:, half * 4:(half + 1) * 4, cl * 128:(cl + 1) * 128], pb_.rearrange("p (a b) -> p a b", a=4), [bpb_], [but])
        store(UT[:, :, sc4 * 512:(sc4 + 1) * 512].rearrange("c p e -> p c e"), ut, but, DB["UT"])
        v_, bv = vl.next()
        load("pool", v_, v_in[sc4 * 512:(sc4 + 1) * 512, :].rearrange("(cl p) d -> p cl d", p=128), bv)
        store(VB[sc4 * 512:(sc4 + 1) * 512, :].rearrange("(cl p) d -> p cl d", p=128), v_, bv, DB["VB"])
    P.barrier()
    st["off"] = PERM_END

    wq, bwq = alloc([8, 2048], BF16, "wq")
    for ch in range(4):
        load("pool", wq[:, :, ch * 512:(ch + 1) * 512], w_q[:, ch * 512:(ch + 1) * 512].rearrange("(c p) n -> p c n", p=128), bwq)
    skT, bskT = alloc([16, 128], BF16, "skT")
    skl = Rot(2, [128])
    for j in range(0, 16, 4):
        pb_, bpb_ = bank()
        for q in range(4):
            sk, bsk = skl.next()
            load("sp", sk, subk[j + q], bsk)
            P.op("pe", lambda e, pb_=pb_, sk=sk, q=q: e.transpose(out=pb_[:, q * 128:(q + 1) * 128], in_=sk, identity=ident), reads=[bsk, b_ident], writes=[bpb_])
        cp("act", skT[:, j:j + 4, :], pb_.rearrange("p (a b) -> p a b", a=4), [bpb_], [bskT])
    x1l = Rot(2, [D])
    h2l = Rot(2, [D])
    h2Tl = Rot(2, [8, 128], BF16)
    qTl = Rot(2, [16, 128], BF16)
    scl = Rot(2, [16, 128])
    wkl = Rot(16, [128])
    tvl = Rot(2, [16, 16])
    til = Rot(2, [16, 16], U32)
    tifl = Rot(2, [2, 128])
    candl = Rot(2, [8, 256])
    cwl = Rot(8, [256])
    el = Rot(2, [8, 256])
    gl_ = Rot(2, [8, 256])
    c16l = Rot(2, [8, 16])
    zl = Rot(2, [24])
    i1Tl = Rot(2, [128])
    i2Tl = Rot(2, [128])
    gTl = Rot(2, [16, 128])
    for ti_ in range(T // 128):
        tsl = slice(ti_ * 128, (ti_ + 1) * 128)
        x1, bx1 = x1l.next()
        load("sp", x1, X1[tsl, :], bx1, extra_reads=[DB["X1"]])
        h2, bh2 = h2l.next()
        tt("pool", h2, x1, s2p1_bc, ALU.mult, [bx1, b_s2], [bh2])
        tt("pool", h2, h2, shift2_bc, ALU.add, [bh2, b_sh2], [bh2])
        h2T, bh2T = h2Tl.next()
        for half in range(2):
            pb_, bpb_ = bank()
            P.op("pe", lambda e, pb_=pb_, h2=h2, half=half: [e.transpose(out=pb_[:, q * 128:(q + 1) * 128], in_=h2[:, (half * 4 + q) * 128:(half * 4 + q + 1) * 128], identity=ident) for q in range(4)][-1],
                 reads=[bh2, b_ident], writes=[bpb_])
            cp("act", h2T[:, half * 4:(half + 1) * 4, :], pb_.rearrange("p (a b) -> p a b", a=4), [bpb_], [bh2T])
        store(H2T[:, :, tsl].rearrange("c p t -> p c t"), h2T, bh2T, DB["H2T"])
        qT, bqT = qTl.next()
        for j4 in range(4):
            pb_, bpb_ = bank()
            mm_group([(pb_[:, q * 128:(q + 1) * 128], wq[:, dc, (j4 * 4 + q) * 128:(j4 * 4 + q + 1) * 128], h2T[:, dc, :], dc == 0, dc == 7) for q in range(4) for dc in range(8)],
                     reads=[bwq, bh2T], writes=[bpb_])
            cp("act", qT[:, j4 * 4:(j4 + 1) * 4, :], pb_.rearrange("p (a b) -> p a b", a=4), [bpb_], [bqT])
        sc, bsc = scl.next()
        for j4 in range(4):
            pb_, bpb_ = bank()
            mm_group([(pb_[:, q * 128:(q + 1) * 128], qT[:, j4 * 4 + q, :], skT[:, j4 * 4 + q, :], True, True) for q in range(4)],
                     reads=[bqT, bskT], writes=[bpb_])
            cp("dve" if j4 % 2 else "act", sc[:, j4 * 4:(j4 + 1) * 4, :], pb_.rearrange("p (a b) -> p a b", a=4), [bpb_], [bsc])
        tv, btv = tvl.next()
        tiu, btiu = til.next()
        tif, btif = tifl.next()
        J = range(16)
        tvb = [Buf("tv%d" % j) for j in J]
        tib = [Buf("ti%d" % j) for j in J]
        wks = [wkl.next() for j in J]
        P.op("dve", lambda e: e.memset(dummy[:, 4:5], 0.0), reads=[], writes=[btv, btiu])
        for j in J:
            P.op("dve", lambda e, tv=tv, sc=sc, j=j: e.max(out=tv[:, j, 0:8], in_=sc[:, j, :]), reads=[bsc, btv], writes=[tvb[j]])
        for j in J:
            P.op("dve", lambda e, tv=tv, sc=sc, j=j, wk=wks[j][0]: e.match_replace(out=wk, in_to_replace=tv[:, j, 0:8], in_values=sc[:, j, :], imm_value=-1e30), reads=[bsc, tvb[j]], writes=[wks[j][1]])
        for j in J:
            P.op("dve", lambda e, tv=tv, j=j, wk=wks[j][0]: e.max(out=tv[:, j, 8:16], in_=wk), reads=[wks[j][1], tvb[j]], writes=[tvb[j]])
        for j in J:
            P.op("dve", lambda e, tv=tv, sc=sc, j=j, tiu=tiu: e.max_index(out=tiu[:, j, 0:8], in_max=tv[:, j, 0:8], in_values=sc[:, j, :]), reads=[bsc, tvb[j], btiu], writes=[tib[j]])
        for j in J:
            P.op("dve", lambda e, tv=tv, sc=sc, j=j, tiu=tiu: e.max_index(out=tiu[:, j, 8:16], in_max=tv[:, j, 8:16], in_values=sc[:, j, :]), reads=[bsc, tvb[j], tib[j]], writes=[tib[j]])
        P.op("dve", lambda e: e.memset(dummy[:, 0:1], 0.0), reads=tvb, writes=[btv])
        P.op("dve", lambda e: e.memset(dummy[:, 1:2], 0.0), reads=tib, writes=[btiu])
        for p_ in range(2):
            cp("pool", V(tif, p_ * 128, [[16, 8], [1, 16]]), V(tiu, p_ * 16, [[32, 8], [1, 16]]), [btiu], [btif])
        cand, bcand = candl.next()
        tt("dve", V(cand, 0, [[256, 8], [16, 16], [1, 16]]), V(tv, 0, [[32, 8], [1, 16], [0, 16]]), V(tv, 16, [[32, 8], [0, 16], [1, 16]]), ALU.add,
           [btv], [bcand])
        c16, bc16 = c16l.next()
        z_, bz = zl.next()
        ee, bee = el.next()
        gg, bgg = gl_.next()
        H8 = range(8)
        cws = [cwl.next() for h in H8]
        c16b = [Buf("c16_%d" % h) for h in H8]
        zb = [Buf("z_%d" % h) for h in H8]
        eeb = [Buf("ee_%d" % h) for h in H8]
        ggb = [Buf("gg_%d" % h) for h in H8]
        P.op("dve", lambda e: e.memset(dummy[:, 5:6], 0.0), reads=[], writes=[bc16, bz, bee, bgg])
        for h in H8:
            P.op("dve", lambda e, c16=c16, cand=cand, h=h: e.max(out=c16[:, h, 0:8], in_=cand[:, h, :]), reads=[bcand, bc16], writes=[c16b[h]])
        for h in H8:
            P.op("dve", lambda e, c16=c16, cand=cand, h=h, cw=cws[h][0]: e.match_replace(out=cw, in_to_replace=c16[:, h, 0:8], in_values=cand[:, h, :], imm_value=-1e30), reads=[bcand, c16b[h]], writes=[cws[h][1]])
        for h in H8:
            P.op("dve", lambda e, c16=c16, h=h, cw=cws[h][0]: e.max(out=c16[:, h, 8:16], in_=cw), reads=[cws[h][1], c16b[h]], writes=[c16b[h]])
        for h in H8:
            ts("dve", z_[:, h:h + 1], c16[:, h, 0:1], -1.0, None, ALU.mult, None, [c16b[h], bz], [zb[h]])
        for h in H8:
            act(ee[:, h, :], cand[:, h, :], AF.Exp, [bcand, zb[h], bee], [eeb[h]], bias=z_[:, h:h + 1])
        for h in H8:
            P.op("dve", lambda e, gg=gg, cand=cand, c16=c16, ee=ee, z_=z_, h=h: e.scalar_tensor_tensor(out=gg[:, h, :], in0=cand[:, h, :], scalar=c16[:, h, 15:16], in1=ee[:, h, :], op0=ALU.is_ge, op1=ALU.mult, accum_out=z_[:, 8 + h:9 + h]),
                 reads=[bcand, c16b[h], eeb[h], zb[h], bgg], writes=[ggb[h], zb[h]])
        P.op("dve", lambda e: e.memset(dummy[:, 2:3], 0.0), reads=zb + c16b, writes=[bz, bc16])
        P.op("dve", lambda e: e.memset(dummy[:, 3:4], 0.0), reads=ggb + eeb, writes=[bgg, bee])
        P.op("dve", lambda e, z_=z_: e.reciprocal(out=z_[:, 16:24], in_=z_[:, 8:16]), reads=[bz], writes=[bz])
        tt("dve", gg, gg, V(z_, 16, [[1, 8], [0, 256]]), ALU.mult, [bgg, bz], [bgg])
        i1T, bi1T = i1Tl.next()
        i2T, bi2T = i2Tl.next()
        pb_, bpb_ = bank()
        P.op("pe", lambda e, pb_=pb_, tif=tif: [e.transpose(out=pb_[:, p_ * 128:(p_ + 1) * 128], in_=tif[:, p_, :], identity=ident) for p_ in range(2)][-1],
             reads=[btif, b_ident], writes=[bpb_])
        cp("act", i1T, pb_[:, 0:128], [bpb_], [bi1T])
        cp("dve", i2T, pb_[:, 128:256], [bpb_], [bi2T])
        store(IDX1T[:, tsl], i1T, bi1T, DB["IDX1T"])
        store(IDX2T[:, tsl], i2T, bi2T, DB["IDX2T"])
        gT, bgT = gTl.next()
        for b4 in range(4):
            pb_, bpb_ = bank()
            P.op("pe", lambda e, pb_=pb_, gg=gg, b4=b4: [e.transpose(out=pb_[:, q * 128:(q + 1) * 128], in_=V(gg, b4 * 4 + q, [[16, 128]]), identity=ident) for q in range(4)][-1],
                 reads=[bgg, b_ident], writes=[bpb_])
            cp("act" if b4 % 2 else "dve", gT[:, b4 * 4:(b4 + 1) * 4, :], pb_.rearrange("p (a b) -> p a b", a=4), [bpb_], [bgT])
        store(GT[:, :, tsl], gT, bgT, DB["GT"])
    P.barrier()
    st["off"] = PERM_END

    Gs, bGs = alloc([256, 128], BF16, "Gs")
    h2g, bh2g = alloc([8, 256], BF16, "h2g")
    i1g, bi1g = alloc([256], F32)
    i2g, bi2g = alloc([256], F32)
    gTg, bgTg = alloc([16, 256], F32)
    TS = 16
    oh1l = Rot(2, [TS, 128], BF16)
    oh2l = Rot(2, [TS, 128], BF16)
    gbl = Rot(2, [TS, 128], BF16)
    apl = Rot(3, [4, 128], BF16)
    utl = Rot(2, [8, 512], BF16)
    vl = Rot(3, [4, D], BF16)
    gel = Rot(4, [256], BF16)
    wl = Rot(4, [256], BF16)
    x1l = Rot(1, [D])
    t1l = Rot(1, [D])
    t2l = Rot(1, [D])
    for gi in range(T // 256):
        g0 = gi * 256
        load("sp", h2g, H2T[:, :, g0:g0 + 256].rearrange("c p t -> p c t"), bh2g, extra_reads=[DB["H2T"]])
        load("sp", i1g, IDX1T[:, g0:g0 + 256], bi1g, extra_reads=[DB["IDX1T"]])
        load("sp", i2g, IDX2T[:, g0:g0 + 256], bi2g, extra_reads=[DB["IDX2T"]])
        load("sp", gTg, GT[:, :, g0:g0 + 256], bgTg, extra_reads=[DB["GT"]])
        gprev = None

        def g_stage2(oh2, boh2, ap_, bap, tok0, par):
            pb2, bpb2 = bank()
            mm_group([(pb2[:, q * 128:(q + 1) * 128], oh2[:, (tok0 % TS) + q, :], ap_[:, q, :], True, True) for q in range(4)],
                     reads=[boh2, bap], writes=[bpb2])
            cp("dve" if par % 2 else "act", Gs[:, tok0:tok0 + 4, :], pb2.rearrange("p (a b) -> p a b", a=4), [bpb2], [bGs])

        for sub in range(256 // TS):
            t0 = sub * TS
            oh1, boh1 = oh1l.next()
            oh2, boh2 = oh2l.next()
            gb, bgb = gbl.next()
            tt("dve", oh1, V(iota, 0, [[0, TS], [1, 128]]), V(i1g, t0, [[1, TS], [0, 128]]), ALU.is_equal, [b_iota, bi1g], [boh1])
            tt("dve", oh2, V(iota, 0, [[0, TS], [1, 128]]), V(i2g, t0, [[1, TS], [0, 128]]), ALU.is_equal, [b_iota, bi2g], [boh2])
            tt("pool", V(gb, 0, [[128, TS], [16, 8], [1, 16]]), V(gTg, t0, [[1, TS], [0, 8], [256, 16]]), V(bm, 0, [[0, TS], [1, 8], [0, 16]]), ALU.mult,
               [bgTg, b_bm], [bgb])
            for t4 in range(TS // 4):
                pb_, bpb_ = bank()
                mm_group([(pb_[:, q * 128:(q + 1) * 128], gb[:, t4 * 4 + q, :], oh1[:, t4 * 4 + q, :], True, True) for q in range(4)],
                         reads=[bgb, boh1], writes=[bpb_])
                ap_, bap = apl.next()
                cp("act", ap_, pb_.rearrange("p (a b) -> p a b", a=4), [bpb_], [bap])
                if gprev is not None:
                    g_stage2(*gprev)
                gprev = (oh2, boh2, ap_, bap, t0 + t4 * 4, t4)
        g_stage2(*gprev)
        gprev = None
        ybanks = [bank() for _ in range(4)]
        ybufs = [b for (_, b) in ybanks]
        pend = []

        def y_stage(w_, bw_, v_, bv, cl, c):
            mm_group([(ybanks[tl * 2 + half][0], w_[:, tl * 128:(tl + 1) * 128], v_[:, cl, half * 512:(half + 1) * 512], c == 0, c == NCH - 1)
                      for tl in range(2) for half in range(2)], reads=[bw_, bv], writes=ybufs)

        for sc4 in range(NCH // 4):
            ut, but = utl.next()
            load("sp", ut, UT[:, :, sc4 * 512:(sc4 + 1) * 512].rearrange("c p e -> p c e"), but, extra_reads=[DB["UT"]])
            v_, bv = vl.next()
            load("sp", v_, VB[sc4 * 512:(sc4 + 1) * 512, :].rearrange("(cl p) d -> p cl d", p=128), bv, extra_reads=[DB["VB"]])
            for cl in range(4):
                c = sc4 * 4 + cl
                ph, bph = bank()
                while bph in ybufs:
                    ph, bph = bank()
                mm_group([(ph[:, 0:256], ut[:, dc, cl * 128:(cl + 1) * 128], h2g[:, dc, :], dc == 0, dc == 7) for dc in range(8)],
                         reads=[but, bh2g], writes=[bph])
                ge, bge = gel.next()
                act(ge, ph[:, 0:256], AF.Gelu, [bph], [bge])
                w_, bw_ = wl.next()
                tt("dve", w_, ge, V(Gs, c, [[128, 256]]), ALU.mult, [bge, bGs], [bw_])
                pend.append((w_, bw_, v_, bv, cl, c))
                if len(pend) > 2:
                    y_stage(*pend.pop(0))
        while pend:
            y_stage(*pend.pop(0))
        for tl in range(2):
            tsl = slice(g0 + tl * 128, g0 + (tl + 1) * 128)
            x1, bx1 = x1l.next()
            load("sp", x1, X1[tsl, :], bx1, extra_reads=[DB["X1"]])
            t1, bt1 = t1l.next()
            t2, bt2 = t2l.next()
            for half in range(2):
                hs = slice(half * 512, (half + 1) * 512)
                yb, byb = ybanks[tl * 2 + half]
                tt("dve", t1[:, hs], yb, gate2_bc[:, hs], ALU.mult, [byb, b_g2], [bt1])
            P.op("dve", lambda e, t1=t1, x1=x1: e.scalar_tensor_tensor(out=t1, in0=x1, scalar=ALPHA, in1=t1, op0=ALU.mult, op1=ALU.add),
                 reads=[bx1, bt1], writes=[bt1])
            layer_norm_tile(t1, bt1, t2, bt2, 2)
            store(out[tsl, :], t2, bt2, DB["out"])
    with nc.allow_low_precision("bf16 matmuls (reference tolerance calibrated for bf16)"):
        P.emit(nc)
    return nc


def host_consts(parity):
    p = np.arange(128)
    c = {}
    c["c_ident"] = np.eye(128, dtype=np.float32)
    c["c_ntri"] = -(p[:, None] >= p[None, :]).astype(np.float32)
    c["c_iota"] = np.tile(np.arange(128, dtype=np.float32)[None, :], (128, 1))
    c["c_bm"] = (p[:, None] // 16 == np.arange(8)[None, :]).astype(np.float32)
    m = np.zeros((128, 2, 4, 512), np.float32)
    q = np.arange(512)
    for a in range(2):
        for kb in range(4):
            kpos = a * 512 + kb * 128 + p
            qpos = parity * 512 + q
            m[:, a, kb, :] = (kpos[:, None] < qpos[None, :]).astype(np.float32)
    c["c_sbmask"] = m.reshape(128, -1)
    slopes = np.array([2.0 ** (-8.0 * (i + 1) / 8) for i in range(8)], np.float32)
    kj = np.arange(256)
    dist = (128 + p[:, None] - kj[None, :]).astype(np.float32)
    valid = (dist >= 0) & (dist < 128)
    sw = np.zeros((128, 2, 8, 256), np.float32)
    for var in range(2):
        vv = valid.copy()
        if var == 0 and parity == 0:
            vv = vv & (kj >= 128)[None, :]
        for h in range(8):
            sw[:, var, h, :] = np.where(vv, -slopes[h] * dist, NEG)
    c["c_swab"] = sw.reshape(128, -1)
    return c


def make_in_maps(inp, NSB):
    S = 512 * NSB
    x = np.asarray(inp["x"], np.float32)
    Bn = x.shape[0]
    f = lambda a: np.ascontiguousarray(np.asarray(a, np.float32))
    common = {
        "w_ada": f(inp["w_ada"][0]), "b_adaT": f(np.asarray(inp["b_ada"][0]).reshape(48, 128).T),
        "w_in": f(inp["w_in"][0]), "sinks": f(np.asarray(inp["swa_sinks"][0]).reshape(1, 8)),
        "gnT": f(np.concatenate([np.asarray(inp["group_norm_a"][0]), np.asarray(inp["group_norm_b"][0])]).reshape(8, 128).T),
        "w_out": f(inp["w_out"][0]),
        "lnp": f(np.stack([np.asarray(inp["ln1_g"][0]), np.asarray(inp["ln1_b"][0]), np.asarray(inp["ln2_g"][0]), np.asarray(inp["ln2_b"][0])])),
        "w_q": f(inp["peer_w_q"][0]), "subk": f(np.asarray(inp["peer_sub_keys"][0]).reshape(16, 128, 128)),
        "u": f(inp["peer_u"][0]), "v": f(inp["peer_v"][0]),
    }
    consts = [host_consts(0), host_consts(1)]
    maps = []
    for core in range(2 * Bn):
        b, par = core // 2, core % 2
        xb = x[b].reshape(NSB, 512, D)
        own = list(range(par, NSB, 2))
        halo = np.zeros((len(own), 128, D), np.float32)
        for k, s in enumerate(own):
            if s > 0:
                halo[k] = xb[s - 1, 384:512]
        m = dict(common)
        m.update(consts[par])
        m["xall"] = f(x[b])
        m["xown"] = f(xb[own].reshape(-1, D))
        m["xhalo"] = f(halo.reshape(-1, D))
        m["cT"] = f(np.asarray(inp["c"])[b].reshape(8, 128).T)
        maps.append(m)
    return maps


def assemble(results, Bn, NSB):
    outp = np.zeros((Bn, NSB, 512, D), np.float32)
    for core in range(2 * Bn):
        b, par = core // 2, core % 2
        own = list(range(par, NSB, 2))
        outp[b, own] = np.asarray(results[core]["out"]).reshape(len(own), 512, D)
    return outp.reshape(Bn, NSB * 512, D)


def kernel(**inputs):
    x = np.asarray(inputs["x"])
    Bn, S, _ = x.shape
    NSB = S // 512
    nc = build(NSB)
    maps = make_in_maps(inputs, NSB)
    res = run_bass_kernel_spmd(nc, maps, core_ids=list(range(2 * Bn)))
    return assemble(res.results, Bn, NSB)
```

```python
import numpy as np
import ml_dtypes
import concourse.bass as bass
import concourse.mybir as mybir
from concourse.bass_utils import run_bass_kernel_spmd

F32 = mybir.dt.float32
BF16 = mybir.dt.bfloat16
U32 = mybir.dt.uint32
AF = mybir.ActivationFunctionType
ALU = mybir.AluOpType
AX = mybir.AxisListType
ENGS = ["pe", "act", "dve", "pool", "sp"]
D = 1024
ALPHA = 2.0 ** 0.25
EPS = 1e-5
NEG = -30000.0


class Buf:
    __slots__ = ("name", "lw", "rd", "dsem", "dcnt")

    def __init__(self, name):
        self.name = name
        self.lw = None
        self.rd = []
        self.dsem = {}
        self.dcnt = {}


class Op:
    __slots__ = ("eng", "fn", "idx", "inc", "is_dma", "dsem", "dval", "waits", "cnt")


class Prog:
    def __init__(self):
        self.ops = {e: [] for e in ENGS}
        self.ndsem = 0
        self.clock = {e: {} for e in ENGS}
        self.dmas = []

    def _mk(self, eng, fn, reads, writes, is_dma, extra=()):
        o = Op()
        o.eng = eng
        o.fn = fn
        o.idx = len(self.ops[eng])
        o.inc = False
        o.is_dma = is_dma
        o.dsem = None
        o.dval = 0
        deps = list(extra)
        for b in reads:
            if b.lw is not None:
                deps.append(b.lw)
        for b in writes:
            if b.lw is not None:
                deps.append(b.lw)
            deps.extend(b.rd)
        ck = self.clock[eng]
        waits = []
        for d in deps:
            if d.is_dma:
                key = ("d", d.dsem)
                if ck.get(key, 0) >= d.dval:
                    continue
                ck[key] = d.dval
                waits.append(("d", d.dsem, d.dval))
            else:
                if d.eng == eng and eng == "pe":
                    continue
                if ck.get(d.eng, -1) >= d.idx:
                    continue
                ck[d.eng] = d.idx
                d.inc = True
                waits.append(("e", d.eng, d))
        o.waits = waits
        for b in reads:
            b.rd.append(o)
        for b in writes:
            b.lw = o
            b.rd = []
        self.ops[eng].append(o)
        return o

    def op(self, eng, fn, reads=(), writes=()):
        return self._mk(eng, fn, reads, writes, False)

    def dma(self, eng, out_ap, in_ap, sb, reads=(), writes=()):
        def fn(e):
            return e.dma_start(out=out_ap, in_=in_ap)
        o = self._mk(eng, fn, reads, writes, True)
        kind = "sw" if eng == "pool" else "hw"
        if kind not in sb.dsem:
            sb.dsem[kind] = self.ndsem
            sb.dcnt[kind] = 0
            self.ndsem += 1
        sb.dcnt[kind] += 16
        o.dsem = sb.dsem[kind]
        o.dval = sb.dcnt[kind]
        self.dmas.append(o)
        return o

    def barrier(self):
        lasts = [self.ops[e][-1] for e in ENGS if self.ops[e] and not self.ops[e][-1].is_dma]
        lasts = []
        for e in ENGS:
            for o in reversed(self.ops[e]):
                if not o.is_dma and o.fn is not None:
                    lasts.append(o)
                    break
        best = {}
        for o in self.dmas:
            if o.dsem not in best or best[o.dsem].dval < o.dval:
                best[o.dsem] = o
        dm = list(best.values())
        self.dmas = []
        for e in ENGS:
            self._mk(e, None, (), (), False, extra=lasts + dm)

    def emit(self, nc):
        self.barrier()
        for e in ENGS:
            c = 0
            for o in self.ops[e]:
                if o.inc and not o.is_dma:
                    c += 1
                o.cnt = c
        import contextlib
        with contextlib.ExitStack() as st:
            esem = {e: st.enter_context(nc.semaphore("s_" + e)) for e in ENGS}
            dsem = [st.enter_context(nc.semaphore("d%d" % i)) for i in range(self.ndsem)]
            block = st.enter_context(nc.Block())
            ops = self.ops

            def run(eng_name, e):
                for o in ops[eng_name]:
                    for w in o.waits:
                        if w[0] == "d":
                            e.wait_ge(dsem[w[1]], w[2])
                        else:
                            e.wait_ge(esem[w[1]], w[2].cnt)
                    if o.fn is None:
                        continue
                    ins = o.fn(e)
                    if o.is_dma:
                        ins.then_inc(dsem[o.dsem], 16)
                    elif o.inc:
                        ins.then_inc(esem[eng_name], 1)

            @block.tensor
            def _(e):
                run("pe", e)

            @block.scalar
            def _(e):
                run("act", e)

            @block.vector
            def _(e):
                run("dve", e)

            @block.gpsimd
            def _(e):
                run("pool", e)

            @block.sync
            def _(e):
                run("sp", e)


def V(a, off, dims):
    return bass.AP(tensor=a.tensor, offset=a.offset + off, ap=[list(a.ap[0])] + [list(d) for d in dims])


def build(NSB, NE=16384, dbg=False):
    S = 512 * NSB
    NOWN = NSB // 2
    T = 512 * NOWN
    NKB = S // 128
    NCH = NE // 128
    nc = bass.Bass("TRN2", target_bir_lowering=False)
    P = Prog()

    def din(name, shape, dt=F32):
        return nc.dram_tensor(name, shape, dt, kind="ExternalInput").ap()

    xall = din("xall", [S, D])
    xown = din("xown", [T, D])
    xhalo = din("xhalo", [NOWN * 128, D])
    cT = din("cT", [128, 8])
    w_ada = din("w_ada", [D, 6 * D])
    b_adaT = din("b_adaT", [128, 48])
    w_in = din("w_in", [D, 2304])
    sinks = din("sinks", [1, 8])
    gnT = din("gnT", [128, 8])
    w_out = din("w_out", [D, D])
    lnp = din("lnp", [4, D])
    w_q = din("w_q", [D, 2048])
    subk = din("subk", [16, 128, 128])
    u_in = din("u", [NE, D])
    v_in = din("v", [NE, D])
    c_ident = din("c_ident", [128, 128])
    c_ntri = din("c_ntri", [128, 128])
    c_iota = din("c_iota", [128, 128])
    c_bm = din("c_bm", [128, 8])
    c_sbmask = din("c_sbmask", [128, 2 * 4 * 512])
    c_swab = din("c_swab", [128, 2 * 8 * 256])
    out = nc.dram_tensor("out", [T, D], F32, kind="ExternalOutput").ap()

    def scratch(name, shape, dt):
        return nc.dram_tensor(name, shape, dt, kind="Internal").ap()

    HTall = scratch("HTall", [8, 128, S], BF16)
    HTown = scratch("HTown", [8, 128, T], BF16)
    HThalo = scratch("HThalo", [8, 128, NOWN * 128], BF16)
    OTs = scratch("OTs", [8, 128, T], BF16)
    X1 = scratch("X1", [T, D], F32)
    H2T = scratch("H2T", [8, 128, T], BF16)
    IDX1T = scratch("IDX1T", [128, T], F32)
    IDX2T = scratch("IDX2T", [128, T], F32)
    GT = scratch("GT", [128, 16, T], F32)
    UT = scratch("UT", [8, 128, NE], BF16)
    VB = scratch("VB", [NE, D], BF16)
    DB = {n: Buf(n) for n in ["HTall", "HTown", "HThalo", "OTs", "X1", "H2T", "IDX1T", "IDX2T", "GT", "UT", "VB", "out"]}

    AW = 52800
    arena = nc.alloc_sbuf_tensor("arena", [128, AW], F32)
    st = {"off": 0, "n": 0}

    def alloc(free_shape, dt=F32, name=None):
        n = int(np.prod(free_shape))
        words = n if dt in (F32, U32) else (n + 1) // 2
        words = (words + 7) // 8 * 8
        off = st["off"]
        st["off"] += words
        assert st["off"] <= AW, ("SBUF arena overflow", st["off"])
        a = arena[:, off:off + words]
        if dt != F32:
            a = a.bitcast(dt)
        a = a[:, 0:n]
        if len(free_shape) == 2:
            a = a.rearrange("p (a b) -> p a b", a=free_shape[0])
        elif len(free_shape) == 3:
            a = a.rearrange("p (a b c) -> p a b c", a=free_shape[0], b=free_shape[1])
        st["n"] += 1
        return a, Buf(name or "b%d" % st["n"])

    class Rot:
        def __init__(self, n, shape, dt=F32):
            self.items = [alloc(shape, dt) for _ in range(n)]
            self.i = 0

        def next(self):
            it = self.items[self.i % len(self.items)]
            self.i += 1
            return it

    ps = [nc.alloc_psum_tensor("ps%d" % i, [128, 512], F32) for i in range(8)]
    PB = [Buf("ps%d" % i) for i in range(8)]
    pst = {"i": 0}

    def bank():
        i = pst["i"] % 8
        pst["i"] += 1
        return ps[i][:, :], PB[i]

    def mm_group(outs, reads, writes):
        def fn(e, outs=outs):
            r = None
            for (o, l, rr, s0, s1) in outs:
                r = e.matmul(o, lhsT=l, rhs=rr, start=s0, stop=s1)
            return r
        P.op("pe", fn, reads=reads, writes=writes)

    def act(out_ap, in_ap, func, reads, writes, bias=0.0, scale=1.0, accum=None):
        def fn(e):
            if accum is not None:
                return e.activation(out=out_ap, in_=in_ap, func=func, bias=bias, scale=scale, accum_out=accum)
            return e.activation(out=out_ap, in_=in_ap, func=func, bias=bias, scale=scale)
        P.op("act", fn, reads=reads, writes=writes)

    def tt(eng, out_ap, a, b, op, reads, writes):
        P.op(eng, lambda e: e.tensor_tensor(out=out_ap, in0=a, in1=b, op=op), reads=reads, writes=writes)

    def ts(eng, out_ap, a, s1, s2, op0, op1, reads, writes, accum=None):
        def fn(e):
            if accum is not None:
                return e.tensor_scalar(out=out_ap, in0=a, scalar1=s1, scalar2=s2, op0=op0, op1=op1, accum_out=accum)
            if op1 is None:
                return e.tensor_scalar(out=out_ap, in0=a, scalar1=s1, scalar2=None, op0=op0)
            return e.tensor_scalar(out=out_ap, in0=a, scalar1=s1, scalar2=s2, op0=op0, op1=op1)
        P.op(eng, fn, reads=reads, writes=writes)

    def cp(eng, out_ap, in_ap, reads, writes):
        if eng == "act":
            P.op("act", lambda e: e.copy(out=out_ap, in_=in_ap), reads=reads, writes=writes)
        else:
            P.op(eng, lambda e: e.tensor_copy(out=out_ap, in_=in_ap), reads=reads, writes=writes)

    def load(eng, dst, src, buf, extra_reads=()):
        P.dma(eng, dst, src, buf, reads=list(extra_reads), writes=[buf])

    def store(dst, src, buf, dbuf):
        P.dma("sp", dst, src, buf, reads=[buf], writes=[dbuf])

    ident, b_ident = alloc([128])
    identb, b_identb = alloc([128], BF16)
    onesf, b_onesf = alloc([128])
    onesb, b_onesb = alloc([128], BF16)
    ntri, b_ntri = alloc([128], BF16)
    nones, b_nones = alloc([128], BF16)
    iota, b_iota = alloc([128])
    bm, b_bm = alloc([8])
    ada, b_ada = alloc([48])
    s1p1, b_s1p1 = alloc([8])
    s2p1c, b_s2p1c = alloc([8])
    gate1_bc, b_g1 = alloc([D])
    s2p1_bc, b_s2 = alloc([D])
    shift2_bc, b_sh2 = alloc([D])
    gate2_bc, b_g2 = alloc([D])
    lnp_bc, b_lnp = alloc([4, D])
    sink_bc, b_sink = alloc([8])
    dummy, b_dummy = alloc([8])

    load("sp", ident, c_ident, b_ident)
    load("pool", identb, c_ident, b_identb)
    load("pool", ntri, c_ntri, b_ntri)
    load("sp", iota, c_iota, b_iota)
    load("sp", bm, c_bm, b_bm)
    P.op("pool", lambda e: e.memset(onesf, 1.0), writes=[b_onesf])
    P.op("pool", lambda e: e.memset(onesb, 1.0), writes=[b_onesb])
    P.op("pool", lambda e: e.memset(nones, -1.0), writes=[b_nones])
    load("sp", lnp_bc, bass.AP(tensor=lnp.tensor, offset=0, ap=[[0, 128], [D, 4], [1, D]]), b_lnp)
    load("sp", sink_bc, bass.AP(tensor=sinks.tensor, offset=0, ap=[[0, 128], [1, 8]]), b_sink)
    PERM_END = st["off"]

    sc_, b_sc = alloc([8])
    csb, b_csb = alloc([8])
    bada, b_bada = alloc([48])
    load("sp", csb, cT, b_csb)
    load("sp", bada, b_adaT, b_bada)
    act(sc_, csb, AF.Silu, [b_csb], [b_sc])
    wa = Rot(2, [8, 1024])
    pa, bpa = bank()
    for gi in range(6):
        wt, bwt = wa.next()
        load("sp", wt, w_ada[:, gi * 1024:(gi + 1) * 1024].rearrange("(c p) n -> p c n", p=128), bwt)
        for jj in range(8):
            j = gi * 8 + jj
            mm_group([(pa[:, j:j + 1], wt[:, dc, jj * 128:(jj + 1) * 128], sc_[:, dc:dc + 1], dc == 0, dc == 7) for dc in range(8)],
                     reads=[bwt, b_sc], writes=[bpa])
    tt("dve", ada, pa[:, 0:48], bada, ALU.add, [bpa, b_bada], [b_ada])
    ts("dve", s1p1, ada[:, 8:16], 1.0, None, ALU.add, None, [b_ada], [b_s1p1])
    ts("dve", s2p1c, ada[:, 32:40], 1.0, None, ALU.add, None, [b_ada], [b_s2p1c])
    shift1 = ada[:, 0:8]
    diag = Rot(2, [128])

    def bcast_row(dst, bdst, col, bcol):
        for half in range(2):
            pb_, bpb_ = bank()
            for q in range(4):
                dc = half * 4 + q
                dg, bdg = diag.next()
                ts("dve", dg, ident, col[:, dc:dc + 1], None, ALU.mult, None, [b_ident, bcol], [bdg])
                mm_group([(pb_[:, q * 128:(q + 1) * 128], onesf, dg, True, True)], reads=[b_onesf, bdg], writes=[bpb_])
            cp("act", dst[:, half * 512:(half + 1) * 512], pb_, [bpb_], [bdst])

    bcast_row(gate1_bc, b_g1, ada[:, 16:24], b_ada)
    bcast_row(s2p1_bc, b_s2, s2p1c, b_s2p1c)
    bcast_row(shift2_bc, b_sh2, ada[:, 24:32], b_ada)
    bcast_row(gate2_bc, b_g2, ada[:, 40:48], b_ada)

    xt = Rot(3, [D])
    hTt = Rot(2, [8, 512], BF16)
    hT_aux = {}

    def make_hT(src, ntok, dst, bdst):
        for t0 in range(0, ntok, 512):
            n = min(512, ntok - t0)
            ht, bht = hTt.next()
            bht2 = hT_aux.setdefault(id(bht), Buf("ht_aux"))
            for s in range(n // 128):
                x_, bx = xt.next()
                load("sp", x_, src[t0 + s * 128:t0 + (s + 1) * 128, :], bx)
                for half in range(2):
                    pb_, bpb_ = bank()
                    P.op("pe", lambda e, pb_=pb_, x_=x_, half=half: [e.transpose(out=pb_[:, q * 128:(q + 1) * 128], in_=x_[:, (half * 4 + q) * 128:(half * 4 + q + 1) * 128], identity=ident) for q in range(4)][-1],
                         reads=[bx, b_ident], writes=[bpb_])
                    for q in range(4):
                        dc = half * 4 + q
                        if True:
                            act(ht[:, dc, s * 128:(s + 1) * 128], pb_[:, q * 128:(q + 1) * 128], AF.Identity, [bpb_, b_s1p1, b_ada], [bht],
                                bias=shift1[:, dc:dc + 1], scale=s1p1[:, dc:dc + 1])
                        else:
                            ts("dve", ht[:, dc, s * 128:(s + 1) * 128], pb_[:, q * 128:(q + 1) * 128], s1p1[:, dc:dc + 1], shift1[:, dc:dc + 1], ALU.mult, ALU.add,
                               [bpb_, b_s1p1, b_ada], [bht2])
            P.dma("sp", dst[:, :, t0:t0 + n].rearrange("c p t -> p c t"), ht[:, :, 0:n], bht, reads=[bht, bht2], writes=[bdst])

    make_hT(xall, S, HTall, DB["HTall"])
    make_hT(xown, T, HTown, DB["HTown"])
    make_hT(xhalo, NOWN * 128, HThalo, DB["HThalo"])
    P.barrier()
    st["off"] = PERM_END

    def wload(cols0, ncols, name):
        w, bw = alloc([8, ncols], BF16, name)
        load("pool", w, w_in[:, cols0:cols0 + ncols].rearrange("(c p) n -> p c n", p=128), bw)
        return w, bw

    def proj_fm(dst, bdst, w, bw, wc0, hT, bhT, t0, n, scale=1.0, eng="act"):
        pb_, bpb_ = bank()
        mm_group([(pb_[:, 0:n], w[:, dc, wc0:wc0 + 128], hT[:, dc, t0:t0 + n], dc == 0, dc == 7) for dc in range(8)],
                 reads=[bw, bhT], writes=[bpb_])
        if eng == "act":
            act(dst, pb_[:, 0:n], AF.Copy, [bpb_], [bdst], scale=scale)
        else:
            cp(eng, dst, pb_[:, 0:n], [bpb_], [bdst])

    wqa, bwqa = wload(0, 512, "wqa")
    wkd, bwkd = alloc([8, 256], BF16, "wkd")
    wvd, bwvd = alloc([8, 256], BF16, "wvd")
    for g in range(2):
        for dup in range(2):
            load("pool", wkd[:, :, (g * 2 + dup) * 64:(g * 2 + dup + 1) * 64], w_in[:, 512 + g * 64:512 + (g + 1) * 64].rearrange("(c p) n -> p c n", p=128), bwkd)
            load("pool", wvd[:, :, (g * 2 + dup) * 64:(g * 2 + dup + 1) * 64], w_in[:, 640 + g * 64:640 + (g + 1) * 64].rearrange("(c p) n -> p c n", p=128), bwvd)
    swab, b_swab = alloc([2, 8, 256], F32, "swab")
    load("sp", swab, c_swab.rearrange("p (a h k) -> p a h k", a=2, h=8), b_swab)
    hTc, bhTc_h = alloc([8, 640], BF16, "hTc")
    bhTc_o = Buf("hTc_o")
    qaT, bqaT = alloc([4, 512], BF16)
    kaT, bkaT = alloc([2, 640], BF16)
    vad, bvad = alloc([5, 256], BF16)
    oTa, boTa = alloc([4, 512], BF16)
    r_sc1 = Rot(8, [256])
    r_p = Rot(8, [256])
    r_pn = Rot(8, [256], BF16)
    r_pT = Rot(8, [2, 128], BF16)
    r_small = Rot(8, [8])
    for i in range(NOWN):
        load("sp", hTc[:, :, 0:128], HThalo[:, :, i * 128:(i + 1) * 128].rearrange("c p t -> p c t"), bhTc_h, extra_reads=[DB["HThalo"]])
        load("sp", hTc[:, :, 128:640], HTown[:, :, i * 512:(i + 1) * 512].rearrange("c p t -> p c t"), bhTc_o, extra_reads=[DB["HTown"]])
        bh = Buf("hTc_both")
        for j in range(4):
            pb_, bpb_ = bank()
            mm_group([(pb_, wqa[:, dc, j * 128:(j + 1) * 128], hTc[:, dc, 128:640], dc == 0, dc == 7) for dc in range(8)],
                     reads=[bwqa, bhTc_o], writes=[bpb_])
            act(qaT[:, j, :], pb_, AF.Copy, [bpb_], [bqaT], scale=0.125)
        for g in range(2):
            for ch in range(2):
                pb_, bpb_ = bank()
                mm_group([(pb_[:, 0:320], wkd[:, dc, g * 128:(g + 1) * 128], hTc[:, dc, ch * 320:(ch + 1) * 320], dc == 0, dc == 7) for dc in range(8)],
                         reads=[bwkd, bhTc_o, bhTc_h], writes=[bpb_])
                cp("dve", kaT[:, g, ch * 320:(ch + 1) * 320], pb_[:, 0:320], [bpb_], [bkaT])
        for blk in range(5):
            pb_, bpb_ = bank()
            mm_group([(pb_[:, 0:256], hTc[:, dc, blk * 128:(blk + 1) * 128], wvd[:, dc, :], dc == 0, dc == 7) for dc in range(8)],
                     reads=[bwvd, bhTc_o, bhTc_h], writes=[bpb_])
            cp("dve", vad[:, blk, :], pb_[:, 0:256], [bpb_], [bvad])
        for r in range(4):
            var = 0 if (i == 0 and r == 0) else 1
            H8 = range(8)
            hb_ = [(h % 2) * 64 for h in H8]
            pr_ = [h // 2 for h in H8]
            g_ = [h // 4 for h in H8]
            pbs = []
            for h in H8:
                pb_, bpb_ = bank()
                mm_group([(pb_[:, 0:256], qaT[hb_[h]:hb_[h] + 64, pr_[h], r * 128:(r + 1) * 128], kaT[hb_[h]:hb_[h] + 64, g_[h], r * 128:r * 128 + 256], True, True)],
                         reads=[bqaT, bkaT], writes=[bpb_])
                pbs.append((pb_, bpb_))
            scs = [r_sc1.next() for h in H8]
            sms = [r_small.next() for h in H8]
            pps = [r_p.next() for h in H8]
            pns = [r_pn.next() for h in H8]
            pTs = [r_pT.next() for h in H8]
            for h in H8:
                tt("dve", scs[h][0], pbs[h][0][:, 0:256], swab[:, var, h, :], ALU.add, [pbs[h][1], b_swab], [scs[h][1]])
            for h in H8:
                P.op("dve", lambda e, sm=sms[h][0], sc1=scs[h][0]: e.reduce_max(out=sm[:, 0:1], in_=sc1, axis=AX.X), reads=[scs[h][1]], writes=[sms[h][1]])
            for h in H8:
                sm, bsm = sms[h]
                ts("dve", sm[:, 1:2], sm[:, 0:1], sink_bc[:, h:h + 1], -1.0, ALU.max, ALU.mult, [bsm, b_sink], [bsm])
            for h in H8:
                sm, bsm = sms[h]
                act(pps[h][0], scs[h][0], AF.Exp, [scs[h][1], bsm], [pps[h][1], bsm], bias=sm[:, 1:2], accum=sm[:, 2:3])
            for h in H8:
                sm, bsm = sms[h]
                act(sm[:, 3:4], sink_bc[:, h:h + 1], AF.Exp, [b_sink, bsm], [bsm], bias=sm[:, 1:2])
            for h in H8:
                sm, bsm = sms[h]
                tt("dve", sm[:, 4:5], sm[:, 2:3], sm[:, 3:4], ALU.add, [bsm], [bsm])
            for h in H8:
                sm, bsm = sms[h]
                P.op("dve", lambda e, sm=sm: e.reciprocal(out=sm[:, 5:6], in_=sm[:, 4:5]), reads=[bsm], writes=[bsm])
            for h in H8:
                sm, bsm = sms[h]
                ts("dve", pns[h][0], pps[h][0], sm[:, 5:6], None, ALU.mult, None, [pps[h][1], bsm], [pns[h][1]])
            pb2s = []
            for h in H8:
                pb2, bpb2 = bank()
                pb2b = pb2.bitcast(BF16)
                P.op("pe", lambda e, pb2b=pb2b, pn=pns[h][0]: [e.transpose(out=pb2b[:, kb * 128:(kb + 1) * 128], in_=pn[:, kb * 128:(kb + 1) * 128], identity=identb) for kb in range(2)][-1],
                     reads=[pns[h][1], b_identb], writes=[bpb2])
                pb2s.append((pb2b, bpb2))
            for h in H8:
                cp("act", pTs[h][0], pb2s[h][0][:, 0:256].rearrange("p (a b) -> p a b", a=2), [pb2s[h][1]], [pTs[h][1]])
            pb3s = []
            for h in H8:
                pb3, bpb3 = bank()
                mm_group([(pb3[:, 0:128], vad[:, r + kb, g_[h] * 128:(g_[h] + 1) * 128], pTs[h][0][:, kb, :], kb == 0, kb == 1) for kb in range(2)],
                         reads=[bvad, pTs[h][1]], writes=[bpb3])
                pb3s.append((pb3, bpb3))
            for h in H8:
                cp("act", oTa[hb_[h]:hb_[h] + 64, pr_[h], r * 128:(r + 1) * 128], pb3s[h][0][hb_[h]:hb_[h] + 64, 0:128], [pb3s[h][1]], [boTa])
        store(OTs[0:4, :, i * 512:(i + 1) * 512].rearrange("c p t -> p c t"), oTa, boTa, DB["OTs"])
    P.barrier()
    st["off"] = PERM_END

    sbmask, b_sbmask = alloc([2, 4, 512], F32, "sbmask")
    load("sp", sbmask, c_sbmask.rearrange("p (a k q) -> p a k q", a=2, k=4), b_sbmask)
    kT, bkT = alloc([2, S], BF16, "kT")
    vv, bvv = alloc([NKB, 256], BF16, "vv")
    A2_END = st["off"]
    for hp2 in range(2):
        st["off"] = A2_END
        wqb, bwqb = wload(768 + hp2 * 256, 256, "wqb")
        wkb, bwkb = wload(1280 + hp2 * 256, 256, "wkb")
        wvb, bwvb = wload(1792 + hp2 * 256, 256, "wvb")
        hTl = Rot(2, [8, 512], BF16)
        for tt_ in range(NSB):
            ht, bht = hTl.next()
            load("sp", ht, HTall[:, :, tt_ * 512:(tt_ + 1) * 512].rearrange("c p t -> p c t"), bht, extra_reads=[DB["HTall"]])
            for pr in range(2):
                proj_fm(kT[:, pr, tt_ * 512:(tt_ + 1) * 512], bkT, wkb, bwkb, pr * 128, ht, bht, 0, 512)
            for s in range(4):
                pb_, bpb_ = bank()
                mm_group([(pb_[:, 0:256], ht[:, dc, s * 128:(s + 1) * 128], wvb[:, dc, :], dc == 0, dc == 7) for dc in range(8)],
                         reads=[bwvb, bht], writes=[bpb_])
                cp("dve", vv[:, tt_ * 4 + s, :], pb_[:, 0:256], [bpb_], [bvv])
        qbT, bqbT = alloc([2, 512], BF16)
        oTb, boTb = alloc([2, 512], BF16)
        r_e = Rot(8, [512])
        r_sp = Rot(12, [512], BF16)
        r_a = Rot(8, [512], BF16)
        Rl = [alloc([512]) for _ in range(4)]
        Rbl = [[alloc([512], BF16) for _ in range(2)] for _ in range(4)]
        for i in range(NOWN):
            ht, bht = hTl.next()
            load("sp", ht, HTown[:, :, i * 512:(i + 1) * 512].rearrange("c p t -> p c t"), bht, extra_reads=[DB["HTown"]])
            for pr in range(2):
                proj_fm(qbT[:, pr, :], bqbT, wqb, bwqb, pr * 128, ht, bht, 0, 512, scale=0.125)
            nkb = 8 * i + 8
            for hl in range(4):
                R, bR = Rl[hl]
                P.op("pool", lambda e, R=R: e.memset(R, 0.0), writes=[bR])
                for pp_ in range(2):
                    Rb, bRb = Rbl[hl][pp_]
                    P.op("pool", lambda e, Rb=Rb: e.memset(Rb, 0.0), writes=[bRb])
            pos = [bank() for _ in range(4)]
            pobufs = [b for (_, b) in pos]

            def fbank():
                while True:
                    p_, b_ = bank()
                    if b_ not in pobufs:
                        return p_, b_

            def stage1(kb):
                masked = kb >= 8 * i
                if masked:
                    mk = sbmask[:, (kb - 8 * i) // 4, (kb - 8 * i) % 4, :]
                sps = []
                pzs = []
                for hl in range(4):
                    pr = hl // 2
                    hb = (hl % 2) * 64
                    ksl = kT[hb:hb + 64, pr, kb * 128:(kb + 1) * 128]
                    qsl = qbT[hb:hb + 64, pr, :]
                    pz, bpz = fbank()
                    mm_group([(pz, ksl, qsl, True, True)], reads=[bkT, bqbT], writes=[bpz])
                    pzs.append((pz, bpz))
                es = []
                for hl in range(4):
                    e_, be = r_e.next()
                    act(e_, pzs[hl][0], AF.Exp, [pzs[hl][1]], [be])
                    es.append((e_, be))
                for hl in range(4):
                    sp_, bsp = r_sp.next()
                    act(sp_, es[hl][0], AF.Ln, [es[hl][1]], [bsp], bias=1.0)
                    sps.append((sp_, bsp))
                if masked:
                    for hl in range(4):
                        sp_, bsp = sps[hl]
                        tt("dve", sp_, sp_, mk, ALU.mult, [bsp, b_sbmask], [bsp])
                return sps

            def stage2(kb, sps):
                masked = kb >= 8 * i
                if masked:
                    mk = sbmask[:, (kb - 8 * i) // 4, (kb - 8 * i) % 4, :]
                avs = []
                for hl in range(4):
                    pr = hl // 2
                    hb = (hl % 2) * 64
                    ksl = kT[hb:hb + 64, pr, kb * 128:(kb + 1) * 128]
                    qsl = qbT[hb:hb + 64, pr, :]
                    sp_, bsp = sps[hl]
                    R, bR = Rl[hl]
                    Rb, bRb = Rbl[hl][kb % 2]
                    Rbn, bRbn = Rbl[hl][(kb + 1) % 2]
                    po, bpo = pos[hl]
                    if kb > 0:
                        tt("dve", Rbn, R, sp_, ALU.add, [bR, bsp], [bRbn])
                        tt("dve", R, R, sp_, ALU.add, [bR, bsp], [bR])
                    pp, bpp = fbank()
                    mm_group([(pp, ksl, qsl, True, False), (pp, ntri, sp_, False, False), (pp, nones, Rb, False, True)],
                             reads=[bkT, bqbT, b_ntri, bsp, b_nones, bRb], writes=[bpp])
                    a_, ba = r_a.next()
                    act(a_, pp, AF.Exp, [bpp], [ba])
                    if masked:
                        tt("pool", a_, a_, mk, ALU.mult, [ba, b_sbmask], [ba])
                    avs.append((po, bpo, pr, a_, ba))
                for (po, bpo, pr, a_, ba) in avs:
                    mm_group([(po, vv[:, kb, pr * 128:(pr + 1) * 128], a_, kb == nkb - 1, kb == 0)], reads=[bvv, ba], writes=[bpo])

            cur = stage1(nkb - 1)
            for kb in range(nkb - 1, -1, -1):
                nxt = stage1(kb - 1) if kb > 0 else None
                stage2(kb, cur)
                cur = nxt
            for hl in range(4):
                pr = hl // 2
                hb = (hl % 2) * 64
                po, bpo = pos[hl]
                cp("act", oTb[hb:hb + 64, pr, :], po[hb:hb + 64, :], [bpo], [boTb])
            store(OTs[4 + hp2 * 2:6 + hp2 * 2, :, i * 512:(i + 1) * 512].rearrange("c p t -> p c t"), oTb, boTb, DB["OTs"])
        P.barrier()
    st["off"] = PERM_END

    wos, bwos = alloc([8, D], BF16, "wos")
    gn, bgn = alloc([8], F32, "gn")
    load("sp", gn, gnT, bgn)
    wstage = Rot(2, [D])
    for ch in range(8):
        wsg, bwsg = wstage.next()
        load("sp", wsg, w_out[ch * 128:(ch + 1) * 128, :], bwsg)
        ts("dve", wos[:, ch, :], wsg, gn[:, ch:ch + 1], None, ALU.mult, None, [bwsg, bgn], [bwos])
    oTl = Rot(2, [8, 512], BF16)
    sql = Rot(2, [8, 512], BF16)
    xl = Rot(2, [D])
    t1l = Rot(2, [D])
    t2l = Rot(2, [D])
    sml = Rot(4, [16])
    stl = Rot(2, [12])

    def layer_norm_tile(y, by, dst, bdst, gi, eng2="pool"):
        stt, bst = stl.next()
        for hh in range(2):
            P.op("dve", lambda e, stt=stt, y=y, hh=hh: e.bn_stats(out=stt[:, hh * 6:(hh + 1) * 6], in_=y[:, hh * 512:(hh + 1) * 512]), reads=[by], writes=[bst])
        sm, bsm = sml.next()
        P.op("dve", lambda e, sm=sm, stt=stt: e.bn_aggr(out=sm[:, 0:2], in_=stt), reads=[bst], writes=[bsm])
        ts("dve", sm[:, 2:3], sm[:, 1:2], EPS, None, ALU.add, None, [bsm], [bsm])
        P.op("act", lambda e, sm=sm: e.sqrt(out=sm[:, 3:4], in_=sm[:, 2:3]), reads=[bsm], writes=[bsm])
        P.op("dve", lambda e, sm=sm: e.reciprocal(out=sm[:, 4:5], in_=sm[:, 3:4]), reads=[bsm], writes=[bsm])
        ts("dve", y, y, sm[:, 0:1], sm[:, 4:5], ALU.subtract, ALU.mult, [by, bsm], [by])
        tt(eng2, y, y, lnp_bc[:, gi, :], ALU.mult, [by, b_lnp], [by])
        tt(eng2, dst, y, lnp_bc[:, gi + 1, :], ALU.add, [by, b_lnp], [bdst])

    for i in range(NOWN):
        oT, boT = oTl.next()
        load("sp", oT, OTs[:, :, i * 512:(i + 1) * 512].rearrange("c p t -> p c t"), boT, extra_reads=[DB["OTs"]])
        sq, bsq = sql.next()
        tt("pool", sq, oT, oT, ALU.mult, [boT], [bsq])
        for s in range(4):
            tok = slice(s * 128, (s + 1) * 128)
            pq, bpq = bank()
            mm_group([(pq[:, grp:grp + 1], sq[:, grp * 4 + ch, tok], onesb[:, 0:1], ch == 0, ch == 3) for grp in range(2) for ch in range(4)],
                     reads=[bsq, b_onesb], writes=[bpq])
            sm, bsm = sml.next()
            ts("dve", sm[:, 0:2], pq[:, 0:2], 1.0 / 512, EPS, ALU.mult, ALU.add, [bpq], [bsm])
            P.op("act", lambda e, sm=sm: e.sqrt(out=sm[:, 2:4], in_=sm[:, 0:2]), reads=[bsm], writes=[bsm])
            P.op("dve", lambda e, sm=sm: e.reciprocal(out=sm[:, 4:6], in_=sm[:, 2:4]), reads=[bsm], writes=[bsm])
            x_, bx = xl.next()
            load("sp", x_, xown[i * 512 + s * 128:i * 512 + (s + 1) * 128, :], bx)
            t1, bt1 = t1l.next()
            t2, bt2 = t2l.next()
            for half in range(2):
                hs = slice(half * 512, (half + 1) * 512)
                pa_, bpa_ = bank()
                mm_group([(pa_, oT[:, ch, tok], wos[:, ch, hs], ch == 0, ch == 3) for ch in range(4)], reads=[boT, bwos], writes=[bpa_])
                pbb, bpbb = bank()
                mm_group([(pbb, oT[:, 4 + ch, tok], wos[:, 4 + ch, hs], ch == 0, ch == 3) for ch in range(4)], reads=[boT, bwos], writes=[bpbb])
                act(t1[:, hs], pa_, AF.Copy, [bpa_, bsm], [bt1], scale=sm[:, 4:5])
                P.op("dve", lambda e, t2=t2, pbb=pbb, sm=sm, t1=t1, hs=hs: e.scalar_tensor_tensor(out=t2[:, hs], in0=pbb, scalar=sm[:, 5:6], in1=t1[:, hs], op0=ALU.mult, op1=ALU.add),
                     reads=[bpbb, bsm, bt1], writes=[bt2])
            tt("pool", t2, t2, gate1_bc, ALU.mult, [bt2, b_g1], [bt2])
            P.op("dve", lambda e, t1=t1, x_=x_, t2=t2: e.scalar_tensor_tensor(out=t1, in0=x_, scalar=ALPHA, in1=t2, op0=ALU.mult, op1=ALU.add),
                 reads=[bx, bt2, bt1], writes=[bt1])
            layer_norm_tile(t1, bt1, t2, bt2, 0)
            store(X1[i * 512 + s * 128:i * 512 + (s + 1) * 128, :], t2, bt2, DB["X1"])
    P.barrier()
    st["off"] = PERM_END

    ul = Rot(2, [D])
    utl = Rot(2, [8, 512], BF16)
    vl = Rot(2, [4, D], BF16)
    for sc4 in range(NCH // 4):
        ut, but = utl.next()
        for cl in range(4):
            c = sc4 * 4 + cl
            u_, bu = ul.next()
            load("sp", u_, u_in[c * 128:(c + 1) * 128, :], bu)
            for half in range(2):
                pb_, bpb_ = bank()
                P.op("pe", lambda e, pb_=pb_, u_=u_, half=half: [e.transpose(out=pb_[:, q * 128:(q + 1) * 128], in_=u_[:, (half * 4 + q) * 128:(half * 4 + q + 1) * 128], identity=ident) for q in range(4)][-1],
                     reads=[bu, b_ident], writes=[bpb_])
                cp("act" if half == 0 else "dve", ut[:, half * 4:(half + 1) * 4, cl * 128:(cl + 1) * 128], pb_.rearrange("p (a b) -> p a b", a=4), [bpb_], [but])
        store(UT[:, :, sc4 * 512:(sc4 + 1) * 512].rearrange("c p e -> p c e"), ut, but, DB["UT"])
        v_, bv = vl.next()
        load("pool", v_, v_in[sc4 * 512:(sc4 + 1) * 512, :].rearrange("(cl p) d -> p cl d", p=128), bv)
        store(VB[sc4 * 512:(sc4 + 1) * 512, :].rearrange("(cl p) d -> p cl d", p=128), v_, bv, DB["VB"])
    P.barrier()
    st["off"] = PERM_END

    wq, bwq = alloc([8, 2048], BF16, "wq")
    for ch in range(4):
        load("pool", wq[:, :, ch * 512:(ch + 1) * 512], w_q[:, ch * 512:(ch + 1) * 512].rearrange("(c p) n -> p c n", p=128), bwq)
    skT, bskT = alloc([16, 128], BF16, "skT")
    skl = Rot(2, [128])
    for j in range(0, 16, 4):
        pb_, bpb_ = bank()
        for q in range(4):
            sk, bsk = skl.next()
            load("sp", sk, subk[j + q], bsk)
            P.op("pe", lambda e, pb_=pb_, sk=sk, q=q: e.transpose(out=pb_[:, q * 128:(q + 1) * 128], in_=sk, identity=ident), reads=[bsk, b_ident], writes=[bpb_])
        cp("act", skT[:, j:j + 4, :], pb_.rearrange("p (a b) -> p a b", a=4), [bpb_], [bskT])
    x1l = Rot(2, [D])
    h2l = Rot(2, [D])
    h2Tl = Rot(2, [8, 128], BF16)
    qTl = Rot(2, [16, 128], BF16)
    scl = Rot(2, [16, 128])
    wkl = Rot(16, [128])
    tvl = Rot(2, [16, 16])
    til = Rot(2, [16, 16], U32)
    tifl = Rot(2, [2, 128])
    candl = Rot(2, [8, 256])
    cwl = Rot(8, [256])
    el = Rot(2, [8, 256])
    gl_ = Rot(2, [8, 256])
    c16l = Rot(2, [8, 16])
    zl = Rot(2, [24])
    i1Tl = Rot(2, [128])
    i2Tl = Rot(2, [128])
    gTl = Rot(2, [16, 128])
    for ti_ in range(T // 128):
        tsl = slice(ti_ * 128, (ti_ + 1) * 128)
        x1, bx1 = x1l.next()
        load("sp", x1, X1[tsl, :], bx1, extra_reads=[DB["X1"]])
        h2, bh2 = h2l.next()
        tt("pool", h2, x1, s2p1_bc, ALU.mult, [bx1, b_s2], [bh2])
        tt("pool", h2, h2, shift2_bc, ALU.add, [bh2, b_sh2], [bh2])
        h2T, bh2T = h2Tl.next()
        for half in range(2):
            pb_, bpb_ = bank()
            P.op("pe", lambda e, pb_=pb_, h2=h2, half=half: [e.transpose(out=pb_[:, q * 128:(q + 1) * 128], in_=h2[:, (half * 4 + q) * 128:(half * 4 + q + 1) * 128], identity=ident) for q in range(4)][-1],
                 reads=[bh2, b_ident], writes=[bpb_])
            cp("act", h2T[:, half * 4:(half + 1) * 4, :], pb_.rearrange("p (a b) -> p a b", a=4), [bpb_], [bh2T])
        store(H2T[:, :, tsl].rearrange("c p t -> p c t"), h2T, bh2T, DB["H2T"])
        qT, bqT = qTl.next()
        for j4 in range(4):
            pb_, bpb_ = bank()
            mm_group([(pb_[:, q * 128:(q + 1) * 128], wq[:, dc, (j4 * 4 + q) * 128:(j4 * 4 + q + 1) * 128], h2T[:, dc, :], dc == 0, dc == 7) for q in range(4) for dc in range(8)],
                     reads=[bwq, bh2T], writes=[bpb_])
            cp("act", qT[:, j4 * 4:(j4 + 1) * 4, :], pb_.rearrange("p (a b) -> p a b", a=4), [bpb_], [bqT])
        sc, bsc = scl.next()
        for j4 in range(4):
            pb_, bpb_ = bank()
            mm_group([(pb_[:, q * 128:(q + 1) * 128], qT[:, j4 * 4 + q, :], skT[:, j4 * 4 + q, :], True, True) for q in range(4)],
                     reads=[bqT, bskT], writes=[bpb_])
            cp("dve" if j4 % 2 else "act", sc[:, j4 * 4:(j4 + 1) * 4, :], pb_.rearrange("p (a b) -> p a b", a=4), [bpb_], [bsc])
        tv, btv = tvl.next()
        tiu, btiu = til.next()
        tif, btif = tifl.next()
        J = range(16)
        tvb = [Buf("tv%d" % j) for j in J]
        tib = [Buf("ti%d" % j) for j in J]
        wks = [wkl.next() for j in J]
        P.op("dve", lambda e: e.memset(dummy[:, 4:5], 0.0), reads=[], writes=[btv, btiu])
        for j in J:
            P.op("dve", lambda e, tv=tv, sc=sc, j=j: e.max(out=tv[:, j, 0:8], in_=sc[:, j, :]), reads=[bsc, btv], writes=[tvb[j]])
        for j in J:
            P.op("dve", lambda e, tv=tv, sc=sc, j=j, wk=wks[j][0]: e.match_replace(out=wk, in_to_replace=tv[:, j, 0:8], in_values=sc[:, j, :], imm_value=-1e30), reads=[bsc, tvb[j]], writes=[wks[j][1]])
        for j in J:
            P.op("dve", lambda e, tv=tv, j=j, wk=wks[j][0]: e.max(out=tv[:, j, 8:16], in_=wk), reads=[wks[j][1], tvb[j]], writes=[tvb[j]])
        for j in J:
            P.op("dve", lambda e, tv=tv, sc=sc, j=j, tiu=tiu: e.max_index(out=tiu[:, j, 0:8], in_max=tv[:, j, 0:8], in_values=sc[:, j, :]), reads=[bsc, tvb[j], btiu], writes=[tib[j]])
        for j in J:
            P.op("dve", lambda e, tv=tv, sc=sc, j=j, tiu=tiu: e.max_index(out=tiu[:, j, 8:16], in_max=tv[:, j, 8:16], in_values=sc[:, j, :]), reads=[bsc, tvb[j], tib[j]], writes=[tib[j]])
        P.op("dve", lambda e: e.memset(dummy[:, 0:1], 0.0), reads=tvb, writes=[btv])
        P.op("dve", lambda e: e.memset(dummy[:, 1:2], 0.0), reads=tib, writes=[btiu])
        for p_ in range(2):
            cp("pool", V(tif, p_ * 128, [[16, 8], [1, 16]]), V(tiu, p_ * 16, [[32, 8], [1, 16]]), [btiu], [btif])
        cand, bcand = candl.next()
        tt("dve", V(cand, 0, [[256, 8], [16, 16], [1, 16]]), V(tv, 0, [[32, 8], [1, 16], [0, 16]]), V(tv, 16, [[32, 8], [0, 16], [1, 16]]), ALU.add,
           [btv], [bcand])
        c16, bc16 = c16l.next()
        z_, bz = zl.next()
        ee, bee = el.next()
        gg, bgg = gl_.next()
        H8 = range(8)
        cws = [cwl.next() for h in H8]
        c16b = [Buf("c16_%d" % h) for h in H8]
        zb = [Buf("z_%d" % h) for h in H8]
        eeb = [Buf("ee_%d" % h) for h in H8]
        ggb = [Buf("gg_%d" % h) for h in H8]
        P.op("dve", lambda e: e.memset(dummy[:, 5:6], 0.0), reads=[], writes=[bc16, bz, bee, bgg])
        for h in H8:
            P.op("dve", lambda e, c16=c16, cand=cand, h=h: e.max(out=c16[:, h, 0:8], in_=cand[:, h, :]), reads=[bcand, bc16], writes=[c16b[h]])
        for h in H8:
            P.op("dve", lambda e, c16=c16, cand=cand, h=h, cw=cws[h][0]: e.match_replace(out=cw, in_to_replace=c16[:, h, 0:8], in_values=cand[:, h, :], imm_value=-1e30), reads=[bcand, c16b[h]], writes=[cws[h][1]])
        for h in H8:
            P.op("dve", lambda e, c16=c16, h=h, cw=cws[h][0]: e.max(out=c16[:, h, 8:16], in_=cw), reads=[cws[h][1], c16b[h]], writes=[c16b[h]])
        for h in H8:
            ts("dve", z_[:, h:h + 1], c16[:, h, 0:1], -1.0, None, ALU.mult, None, [c16b[h], bz], [zb[h]])
        for h in H8:
            act(ee[:, h, :], cand[:, h, :], AF.Exp, [bcand, zb[h], bee], [eeb[h]], bias=z_[:, h:h + 1])
        for h in H8:
            P.op("dve", lambda e, gg=gg, cand=cand, c16=c16, ee=ee, z_=z_, h=h: e.scalar_tensor_tensor(out=gg[:, h, :], in0=cand[:, h, :], scalar=c16[:, h, 15:16], in1=ee[:, h, :], op0=ALU.is_ge, op1=ALU.mult, accum_out=z_[:, 8 + h:9 + h]),
                 reads=[bcand, c16b[h], eeb[h], zb[h], bgg], writes=[ggb[h], zb[h]])
        P.op("dve", lambda e: e.memset(dummy[:, 2:3], 0.0), reads=zb + c16b, writes=[bz, bc16])
        P.op("dve", lambda e: e.memset(dummy[:, 3:4], 0.0), reads=ggb + eeb, writes=[bgg, bee])
        P.op("dve", lambda e, z_=z_: e.reciprocal(out=z_[:, 16:24], in_=z_[:, 8:16]), reads=[bz], writes=[bz])
        tt("dve", gg, gg, V(z_, 16, [[1, 8], [0, 256]]), ALU.mult, [bgg, bz], [bgg])
        i1T, bi1T = i1Tl.next()
        i2T, bi2T = i2Tl.next()
        pb_, bpb_ = bank()
        P.op("pe", lambda e, pb_=pb_, tif=tif: [e.transpose(out=pb_[:, p_ * 128:(p_ + 1) * 128], in_=tif[:, p_, :], identity=ident) for p_ in range(2)][-1],
             reads=[btif, b_ident], writes=[bpb_])
        cp("act", i1T, pb_[:, 0:128], [bpb_], [bi1T])
        cp("dve", i2T, pb_[:, 128:256], [bpb_], [bi2T])
        store(IDX1T[:, tsl], i1T, bi1T, DB["IDX1T"])
        store(IDX2T[:, tsl], i2T, bi2T, DB["IDX2T"])
        gT, bgT = gTl.next()
        for b4 in range(4):
            pb_, bpb_ = bank()
            P.op("pe", lambda e, pb_=pb_, gg=gg, b4=b4: [e.transpose(out=pb_[:, q * 128:(q + 1) * 128], in_=V(gg, b4 * 4 + q, [[16, 128]]), identity=ident) for q in range(4)][-1],
                 reads=[bgg, b_ident], writes=[bpb_])
            cp("act" if b4 % 2 else "dve", gT[:, b4 * 4:(b4 + 1) * 4, :], pb_.rearrange("p (a b) -> p a b", a=4), [bpb_], [bgT])
        store(GT[:, :, tsl], gT, bgT, DB["GT"])
    P.barrier()
    st["off"] = PERM_END

    Gs, bGs = alloc([256, 128], BF16, "Gs")
    h2g, bh2g = alloc([8, 256], BF16, "h2g")
    i1g, bi1g = alloc([256], F32)
    i2g, bi2g = alloc([256], F32)
    gTg, bgTg = alloc([16, 256], F32)
    TS = 16
    oh1l = Rot(2, [TS, 128], BF16)
    oh2l = Rot(2, [TS, 128], BF16)
    gbl = Rot(2, [TS, 128], BF16)
    apl = Rot(3, [4, 128], BF16)
    utl = Rot(2, [8, 512], BF16)
    vl = Rot(3, [4, D], BF16)
    gel = Rot(4, [256], BF16)
    wl = Rot(4, [256], BF16)
    x1l = Rot(1, [D])
    t1l = Rot(1, [D])
    t2l = Rot(1, [D])
    for gi in range(T // 256):
        g0 = gi * 256
        load("sp", h2g, H2T[:, :, g0:g0 + 256].rearrange("c p t -> p c t"), bh2g, extra_reads=[DB["H2T"]])
        load("sp", i1g, IDX1T[:, g0:g0 + 256], bi1g, extra_reads=[DB["IDX1T"]])
        load("sp", i2g, IDX2T[:, g0:g0 + 256], bi2g, extra_reads=[DB["IDX2T"]])
        load("sp", gTg, GT[:, :, g0:g0 + 256], bgTg, extra_reads=[DB["GT"]])
        gprev = None

        def g_stage2(oh2, boh2, ap_, bap, tok0, par):
            pb2, bpb2 = bank()
            mm_group([(pb2[:, q * 128:(q + 1) * 128], oh2[:, (tok0 % TS) + q, :], ap_[:, q, :], True, True) for q in range(4)],
                     reads=[boh2, bap], writes=[bpb2])
            cp("dve" if par % 2 else "act", Gs[:, tok0:tok0 + 4, :], pb2.rearrange("p (a b) -> p a b", a=4), [bpb2], [bGs])

        for sub in range(256 // TS):
            t0 = sub * TS
            oh1, boh1 = oh1l.next()
            oh2, boh2 = oh2l.next()
            gb, bgb = gbl.next()
            tt("dve", oh1, V(iota, 0, [[0, TS], [1, 128]]), V(i1g, t0, [[1, TS], [0, 128]]), ALU.is_equal, [b_iota, bi1g], [boh1])
            tt("dve", oh2, V(iota, 0, [[0, TS], [1, 128]]), V(i2g, t0, [[1, TS], [0, 128]]), ALU.is_equal, [b_iota, bi2g], [boh2])
            tt("pool", V(gb, 0, [[128, TS], [16, 8], [1, 16]]), V(gTg, t0, [[1, TS], [0, 8], [256, 16]]), V(bm, 0, [[0, TS], [1, 8], [0, 16]]), ALU.mult,
               [bgTg, b_bm], [bgb])
            for t4 in range(TS // 4):
                pb_, bpb_ = bank()
                mm_group([(pb_[:, q * 128:(q + 1) * 128], gb[:, t4 * 4 + q, :], oh1[:, t4 * 4 + q, :], True, True) for q in range(4)],
                         reads=[bgb, boh1], writes=[bpb_])
                ap_, bap = apl.next()
                cp("act", ap_, pb_.rearrange("p (a b) -> p a b", a=4), [bpb_], [bap])
                if gprev is not None:
                    g_stage2(*gprev)
                gprev = (oh2, boh2, ap_, bap, t0 + t4 * 4, t4)
        g_stage2(*gprev)
        gprev = None
        ybanks = [bank() for _ in range(4)]
        ybufs = [b for (_, b) in ybanks]
        pend = []

        def y_stage(w_, bw_, v_, bv, cl, c):
            mm_group([(ybanks[tl * 2 + half][0], w_[:, tl * 128:(tl + 1) * 128], v_[:, cl, half * 512:(half + 1) * 512], c == 0, c == NCH - 1)
                      for tl in range(2) for half in range(2)], reads=[bw_, bv], writes=ybufs)

        for sc4 in range(NCH // 4):
            ut, but = utl.next()
            load("sp", ut, UT[:, :, sc4 * 512:(sc4 + 1) * 512].rearrange("c p e -> p c e"), but, extra_reads=[DB["UT"]])
            v_, bv = vl.next()
            load("sp", v_, VB[sc4 * 512:(sc4 + 1) * 512, :].rearrange("(cl p) d -> p cl d", p=128), bv, extra_reads=[DB["VB"]])
            for cl in range(4):
                c = sc4 * 4 + cl
                ph, bph = bank()
                while bph in ybufs:
                    ph, bph = bank()
                mm_group([(ph[:, 0:256], ut[:, dc, cl * 128:(cl + 1) * 128], h2g[:, dc, :], dc == 0, dc == 7) for dc in range(8)],
                         reads=[but, bh2g], writes=[bph])
                ge, bge = gel.next()
                act(ge, ph[:, 0:256], AF.Gelu, [bph], [bge])
                w_, bw_ = wl.next()
                tt("dve", w_, ge, V(Gs, c, [[128, 256]]), ALU.mult, [bge, bGs], [bw_])
                pend.append((w_, bw_, v_, bv, cl, c))
                if len(pend) > 2:
                    y_stage(*pend.pop(0))
        while pend:
            y_stage(*pend.pop(0))
        for tl in range(2):
            tsl = slice(g0 + tl * 128, g0 + (tl + 1) * 128)
            x1, bx1 = x1l.next()
            load("sp", x1, X1[tsl, :], bx1, extra_reads=[DB["X1"]])
            t1, bt1 = t1l.next()
            t2, bt2 = t2l.next()
            for half in range(2):
                hs = slice(half * 512, (half + 1) * 512)
                yb, byb = ybanks[tl * 2 + half]
                tt("dve", t1[:, hs], yb, gate2_bc[:, hs], ALU.mult, [byb, b_g2], [bt1])
            P.op("dve", lambda e, t1=t1, x1=x1: e.scalar_tensor_tensor(out=t1, in0=x1, scalar=ALPHA, in1=t1, op0=ALU.mult, op1=ALU.add),
                 reads=[bx1, bt1], writes=[bt1])
            layer_norm_tile(t1, bt1, t2, bt2, 2)
            store(out[tsl, :], t2, bt2, DB["out"])
    with nc.allow_low_precision("bf16 matmuls (reference tolerance calibrated for bf16)"):
        P.emit(nc)
    return nc


def host_consts(parity):
    p = np.arange(128)
    c = {}
    c["c_ident"] = np.eye(128, dtype=np.float32)
    c["c_ntri"] = -(p[:, None] >= p[None, :]).astype(np.float32)
    c["c_iota"] = np.tile(np.arange(128, dtype=np.float32)[None, :], (128, 1))
    c["c_bm"] = (p[:, None] // 16 == np.arange(8)[None, :]).astype(np.float32)
    m = np.zeros((128, 2, 4, 512), np.float32)
    q = np.arange(512)
    for a in range(2):
        for kb in range(4):
            kpos = a * 512 + kb * 128 + p
            qpos = parity * 512 + q
            m[:, a, kb, :] = (kpos[:, None] < qpos[None, :]).astype(np.float32)
    c["c_sbmask"] = m.reshape(128, -1)
    slopes = np.array([2.0 ** (-8.0 * (i + 1) / 8) for i in range(8)], np.float32)
    kj = np.arange(256)
    dist = (128 + p[:, None] - kj[None, :]).astype(np.float32)
    valid = (dist >= 0) & (dist < 128)
    sw = np.zeros((128, 2, 8, 256), np.float32)
    for var in range(2):
        vv = valid.copy()
        if var == 0 and parity == 0:
            vv = vv & (kj >= 128)[None, :]
        for h in range(8):
            sw[:, var, h, :] = np.where(vv, -slopes[h] * dist, NEG)
    c["c_swab"] = sw.reshape(128, -1)
    return c


def make_in_maps(inp, NSB):
    S = 512 * NSB
    x = np.asarray(inp["x"], np.float32)
    Bn = x.shape[0]
    f = lambda a: np.ascontiguousarray(np.asarray(a, np.float32))
    common = {
        "w_ada": f(inp["w_ada"][0]), "b_adaT": f(np.asarray(inp["b_ada"][0]).reshape(48, 128).T),
        "w_in": f(inp["w_in"][0]), "sinks": f(np.asarray(inp["swa_sinks"][0]).reshape(1, 8)),
        "gnT": f(np.concatenate([np.asarray(inp["group_norm_a"][0]), np.asarray(inp["group_norm_b"][0])]).reshape(8, 128).T),
        "w_out": f(inp["w_out"][0]),
        "lnp": f(np.stack([np.asarray(inp["ln1_g"][0]), np.asarray(inp["ln1_b"][0]), np.asarray(inp["ln2_g"][0]), np.asarray(inp["ln2_b"][0])])),
        "w_q": f(inp["peer_w_q"][0]), "subk": f(np.asarray(inp["peer_sub_keys"][0]).reshape(16, 128, 128)),
        "u": f(inp["peer_u"][0]), "v": f(inp["peer_v"][0]),
    }
    consts = [host_consts(0), host_consts(1)]
    maps = []
    for core in range(2 * Bn):
        b, par = core // 2, core % 2
        xb = x[b].reshape(NSB, 512, D)
        own = list(range(par, NSB, 2))
        halo = np.zeros((len(own), 128, D), np.float32)
        for k, s in enumerate(own):
            if s > 0:
                halo[k] = xb[s - 1, 384:512]
        m = dict(common)
        m.update(consts[par])
        m["xall"] = f(x[b])
        m["xown"] = f(xb[own].reshape(-1, D))
        m["xhalo"] = f(halo.reshape(-1, D))
        m["cT"] = f(np.asarray(inp["c"])[b].reshape(8, 128).T)
        maps.append(m)
    return maps


def assemble(results, Bn, NSB):
    outp = np.zeros((Bn, NSB, 512, D), np.float32)
    for core in range(2 * Bn):
        b, par = core // 2, core % 2
        own = list(range(par, NSB, 2))
        outp[b, own] = np.asarray(results[core]["out"]).reshape(len(own), 512, D)
    return outp.reshape(Bn, NSB * 512, D)


def kernel(**inputs):
    x = np.asarray(inputs["x"])
    Bn, S, _ = x.shape
    NSB = S // 512
    nc = build(NSB)
    maps = make_in_maps(inputs, NSB)
    res = run_bass_kernel_spmd(nc, maps, core_ids=list(range(2 * Bn)))
    return assemble(res.results, Bn, NSB)
```
